# Optimizing a Trainium2 kernel written in Bass

```python
import jax, jax.numpy as jnp
from jax import lax
import numpy as np

D_MODEL = 2048
BATCH = 4
SEQ = 2048
DEPTH = 2
DEC_BATCH = 128
DEC_SEQ = 4
PAST_LEN = 16384
PAGE_SIZE = 128

D_CONV = D_MODEL // 2
CONV_A_WIDTH = 3
N_HEADS = 8
HEAD_DIM = 128
D_GDN = N_HEADS * HEAD_DIM
GDN_CONV_WIDTH = 4
CHUNK = 64
D_FF = 4 * D_MODEL
ALPHA = (2 * DEPTH) ** 0.25
BETA_INIT = (8 * DEPTH) ** -0.25
LN_EPS = 1e-5
NORM_EPS = 1e-6
SPLITS = (D_CONV, D_CONV, D_CONV, D_GDN, D_GDN, D_GDN, D_GDN, N_HEADS, N_HEADS, D_MODEL, D_MODEL)
D_PROJ = sum(SPLITS)

kernel_name = 'hybrid_shortconv_gdn_deepnorm_step'


def layer_norm(x, g, b):
    xf = x.astype(jnp.float32)
    mu = jnp.mean(xf, -1, keepdims=True)
    var = jnp.mean(jnp.square(xf - mu), -1, keepdims=True)
    return ((xf - mu) * lax.rsqrt(var + LN_EPS) * g.astype(jnp.float32) + b.astype(jnp.float32)).astype(x.dtype)


def l2norm(x):
    xf = x.astype(jnp.float32)
    return xf * lax.rsqrt(jnp.sum(xf * xf, -1, keepdims=True) + NORM_EPS)


def causal_depthwise_conv(u, buf, w):
    K = w.shape[0]
    L = u.shape[1]
    full = jnp.concatenate([buf.astype(u.dtype), u], axis=1)
    out = sum(full[:, j:j + L] * w[j] for j in range(K))
    return out, full[:, -(K - 1):]


def gated_delta_rule(q, k, v, g, beta, S0):
    Bsz, L, H, DK = q.shape
    DV = v.shape[-1]
    C = min(CHUNK, L)
    Lp = -(-L // C) * C
    pad = Lp - L
    if pad:
        padf = lambda a: jnp.pad(a, [(0, 0), (0, pad)] + [(0, 0)] * (a.ndim - 2))
        q, k, v, g, beta = padf(q), padf(k), padf(v), padf(g), padf(beta)
    N = Lp // C

    def chunks(a):
        return a.reshape(Bsz, N, C, H, -1).transpose(1, 0, 3, 2, 4)

    qc = chunks(q * (DK ** -0.5))
    kc = chunks(k)
    vc = chunks(v)
    gc = chunks(g[..., None])[..., 0]
    bc = chunks(beta[..., None])
    gcum = jnp.cumsum(gc, axis=-1)
    idx = jnp.arange(C)
    causal = idx[:, None] >= idx[None, :]
    strict = idx[:, None] > idx[None, :]
    diff = gcum[..., :, None] - gcum[..., None, :]
    decay = jnp.where(causal, jnp.exp(jnp.where(causal, diff, 0.0)), 0.0)
    kb = kc * bc
    M = jnp.where(strict, jnp.einsum('nbhid,nbhjd->nbhij', kb, kc) * decay, 0.0)
    eye = jnp.eye(C, dtype=jnp.float32)
    T = lax.linalg.triangular_solve(eye + M, jnp.broadcast_to(eye, M.shape),
                                    left_side=True, lower=True, unit_diagonal=True)
    u = T @ (vc * bc)
    w = T @ (kb * jnp.exp(gcum)[..., None])
    attn = jnp.einsum('nbhid,nbhjd->nbhij', qc, kc) * decay
    glast = gcum[..., -1:]
    kdec = kc * jnp.exp(glast - gcum)[..., None]
    qdec = qc * jnp.exp(gcum)[..., None]

    def step(S, xs):
        qd, kd, u_c, w_c, a_c, gl = xs
        v_new = u_c - w_c @ S
        o = qd @ S + a_c @ v_new
        S = S * jnp.exp(gl)[..., None] + jnp.einsum('bhcd,bhce->bhde', kd, v_new)
        return S, o

    S, o = lax.scan(step, S0, (qdec, kdec, u, w, attn, glast))
    o = o.transpose(1, 0, 3, 2, 4).reshape(Bsz, Lp, H, DV)[:, :L]
    return o, S


def hybrid_layer(x, conv_a_buf, gdn_buf, S0, w_in, conv_a_w, gdn_conv_w, a_log, dt_bias, gdn_norm_w,
                 w_a_out, w_b_out, w_o, ln1_g, ln1_b, w_up, w_down, ln2_g, ln2_b):
    Bsz, L, _ = x.shape
    proj = x @ w_in
    split_points = np.cumsum(SPLITS)[:-1].tolist()
    gB, gC, h, q, k, v, z, b_logit, a_logit, m_a, m_b = jnp.split(proj, split_points, axis=-1)

    conv_out, new_conv_a = causal_depthwise_conv(gC * h, conv_a_buf, conv_a_w)
    y_a = (gB * conv_out) @ w_a_out

    qkv, new_gdn_buf = causal_depthwise_conv(jnp.concatenate([q, k, v], -1), gdn_buf, gdn_conv_w)
    q, k, v = jnp.split(jax.nn.silu(qkv), 3, axis=-1)
    hs = (Bsz, L, N_HEADS, HEAD_DIM)
    qh = l2norm(q.reshape(hs))
    kh = l2norm(k.reshape(hs))
    vh = v.reshape(hs).astype(jnp.float32)
    beta = jax.nn.sigmoid(b_logit.astype(jnp.float32))
    g = -jnp.exp(a_log.astype(jnp.float32)) * jax.nn.softplus(a_logit.astype(jnp.float32) + dt_bias.astype(jnp.float32))
    o, S = gated_delta_rule(qh, kh, vh, g, beta, S0.astype(jnp.float32))
    o = o * lax.rsqrt(jnp.mean(o * o, -1, keepdims=True) + NORM_EPS) * gdn_norm_w.astype(jnp.float32)
    o = o * jax.nn.silu(z.reshape(hs).astype(jnp.float32))
    y_b = o.reshape(Bsz, L, D_GDN).astype(x.dtype) @ w_b_out

    merged = jax.nn.sigmoid(m_a) * y_a + jax.nn.sigmoid(m_b) * y_b
    x = layer_norm(ALPHA * x + merged @ w_o, ln1_g, ln1_b)
    hid = jnp.square(jax.nn.relu(x @ w_up))
    x = layer_norm(ALPHA * x + hid @ w_down, ln2_g, ln2_b)
    return x, new_conv_a, new_gdn_buf, S.astype(S0.dtype)


def run_trunk(x, st_conv_a, st_gdn_conv, st_gdn, w_in, conv_a_w, gdn_conv_w, a_log, dt_bias, gdn_norm_w,
              w_a_out, w_b_out, w_o, ln1_g, ln1_b, w_up, w_down, ln2_g, ln2_b):
    ca, cg, sg = [], [], []
    for l in range(DEPTH):
        x, c1, c2, s = hybrid_layer(x, st_conv_a[l], st_gdn_conv[l], st_gdn[l], w_in[l], conv_a_w[l], gdn_conv_w[l],
                                    a_log[l], dt_bias[l], gdn_norm_w[l], w_a_out[l], w_b_out[l], w_o[l],
                                    ln1_g[l], ln1_b[l], w_up[l], w_down[l], ln2_g[l], ln2_b[l])
        ca.append(c1)
        cg.append(c2)
        sg.append(s)
    return x, jnp.stack(ca), jnp.stack(cg), jnp.stack(sg)


def setup_inputs(seed: int = 0) -> dict:
    key = jax.random.key(seed)
    ks = jax.random.split(key, 24)
    f32 = jnp.float32
    nrm = lambda k, shape, s: jax.random.normal(k, shape, f32) * s
    col_scale = jnp.concatenate([
        jnp.ones((2 * D_CONV,), f32), jnp.full((D_CONV,), BETA_INIT, f32),
        jnp.ones((2 * D_GDN,), f32), jnp.full((D_GDN,), BETA_INIT, f32),
        jnp.ones((D_GDN + 2 * N_HEADS + 2 * D_MODEL,), f32)])
    dt = jnp.exp(jax.random.uniform(ks[5], (DEPTH, N_HEADS), f32, np.log(1e-3), np.log(1e-1)))
    return {
        'x_prompt': nrm(ks[0], (BATCH, SEQ, D_MODEL), 1.0),
        'x_sample': nrm(ks[1], (DEC_BATCH, DEC_SEQ, D_MODEL), 1.0),
        'state_conv_a': nrm(ks[2], (DEPTH, DEC_BATCH, CONV_A_WIDTH - 1, D_CONV), 1.0),
        'state_gdn_conv': nrm(ks[3], (DEPTH, DEC_BATCH, GDN_CONV_WIDTH - 1, 3 * D_GDN), 1.0),
        'state_gdn': nrm(ks[4], (DEPTH, DEC_BATCH, N_HEADS, HEAD_DIM, HEAD_DIM), 0.1),
        'w_in': nrm(ks[6], (DEPTH, D_MODEL, D_PROJ), D_MODEL ** -0.5) * col_scale,
        'conv_a_w': nrm(ks[7], (DEPTH, CONV_A_WIDTH, D_CONV), CONV_A_WIDTH ** -0.5),
        'gdn_conv_w': nrm(ks[8], (DEPTH, GDN_CONV_WIDTH, 3 * D_GDN), GDN_CONV_WIDTH ** -0.5),
        'a_log': jnp.log(jax.random.uniform(ks[9], (DEPTH, N_HEADS), f32, 1.0, 16.0)),
        'dt_bias': dt + jnp.log(-jnp.expm1(-dt)),
        'gdn_norm_w': 1.0 + nrm(ks[10], (DEPTH, HEAD_DIM), 0.02),
        'w_a_out': nrm(ks[11], (DEPTH, D_CONV, D_MODEL), BETA_INIT * D_CONV ** -0.5),
        'w_b_out': nrm(ks[12], (DEPTH, D_GDN, D_MODEL), BETA_INIT * D_GDN ** -0.5),
        'w_o': nrm(ks[13], (DEPTH, D_MODEL, D_MODEL), BETA_INIT * D_MODEL ** -0.5),
        'ln1_g': 1.0 + nrm(ks[14], (DEPTH, D_MODEL), 0.02),
        'ln1_b': nrm(ks[15], (DEPTH, D_MODEL), 0.02),
        'w_up': nrm(ks[16], (DEPTH, D_MODEL, D_FF), BETA_INIT * D_MODEL ** -0.5),
        'w_down': nrm(ks[17], (DEPTH, D_FF, D_MODEL), BETA_INIT * D_FF ** -0.5),
        'ln2_g': 1.0 + nrm(ks[18], (DEPTH, D_MODEL), 0.02),
        'ln2_b': nrm(ks[19], (DEPTH, D_MODEL), 0.02),
    }


def reference(x_prompt, x_sample, state_conv_a, state_gdn_conv, state_gdn, w_in, conv_a_w, gdn_conv_w, a_log,
              dt_bias, gdn_norm_w, w_a_out, w_b_out, w_o, ln1_g, ln1_b, w_up, w_down, ln2_g, ln2_b):
    bp = x_prompt.shape[0]
    dtp = x_prompt.dtype
    z_conv_a = jnp.zeros((DEPTH, bp, CONV_A_WIDTH - 1, D_CONV), dtp)
    z_gdn_conv = jnp.zeros((DEPTH, bp, GDN_CONV_WIDTH - 1, 3 * D_GDN), dtp)
    z_gdn = jnp.zeros((DEPTH, bp, N_HEADS, HEAD_DIM, HEAD_DIM), dtp)
    y_prompt, new_conv_a_prompt, new_gdn_conv_prompt, new_gdn_prompt = run_trunk(
        x_prompt, z_conv_a, z_gdn_conv, z_gdn, w_in, conv_a_w, gdn_conv_w, a_log, dt_bias, gdn_norm_w,
        w_a_out, w_b_out, w_o, ln1_g, ln1_b, w_up, w_down, ln2_g, ln2_b)
    y_sample, new_conv_a_sample, new_gdn_conv_sample, new_gdn_sample = run_trunk(
        x_sample, state_conv_a, state_gdn_conv, state_gdn, w_in, conv_a_w, gdn_conv_w, a_log, dt_bias, gdn_norm_w,
        w_a_out, w_b_out, w_o, ln1_g, ln1_b, w_up, w_down, ln2_g, ln2_b)
    return (y_prompt, y_sample, new_conv_a_prompt, new_gdn_conv_prompt, new_gdn_prompt,
            new_conv_a_sample, new_gdn_conv_sample, new_gdn_sample)
```

```python
import numpy as np
from contextlib import ExitStack
import concourse.bass as bass
import concourse.mybir as mybir
from concourse.bass_utils import run_bass_kernel_spmd

F32 = mybir.dt.float32
BF16 = mybir.dt.bfloat16
AF = mybir.ActivationFunctionType
ALU = mybir.AluOpType
AX = mybir.AxisListType

L = 2
D = 2048
NCORES = 8
ALPHA = float((2 * L) ** 0.25)
LN_EPS = 1e-5
NORM_EPS = 1e-6
NPASS = 4
PT = 512
NS = 64
NTMAX = PT + NS
NEG = -30000.0
O_GB, O_GC, O_H, O_Q, O_K, O_V, O_Z, O_B, O_A, O_MA, O_MB = 0, 1024, 2048, 3072, 4096, 5120, 6144, 7168, 7176, 7184, 9232


def tile_plan():
    tiles = []
    for h in range(8):
        for off in (O_Q, O_K, O_V, O_Z):
            tiles.append(("w_in", [(kc, (off // 128) + h) for kc in range(16)]))
    for cc in range(8):
        for off in (O_GB, O_GC, O_H):
            tiles.append(("w_in", [(kc, (off // 128) + cc) for kc in range(16)]))
    for oc in range(16):
        tiles.append(("w_a_out", [(kc, oc) for kc in range(8)]))
        tiles.append(("w_b_out", [(kc, oc) for kc in range(8)]))
        tiles.append(("w_in_ma", [(kc, oc) for kc in range(16)]))
        tiles.append(("w_in_mb", [(kc, oc) for kc in range(16)]))
    for oc in range(16):
        tiles.append(("w_o", [(kc, oc) for kc in range(16)]))
    for fc in range(64):
        tiles.append(("w_up", [(kc, fc) for kc in range(16)]))
    for oc in range(16):
        for g in range(4):
            tiles.append(("w_down", [(g * 16 + kc, oc) for kc in range(16)]))
    return tiles


PLAN = tile_plan()
NTILES = len(PLAN)
TILE_ID = {}
for _i, (_m, _b) in enumerate(PLAN):
    TILE_ID[(_m, _b[0][1], _b[0][0] // 16)] = _i


DBG = {"stop": None}
NW = [0]


class StopBuild(Exception):
    pass


def ckpt(name):
    if DBG["stop"] == name:
        raise StopBuild(name)


class Tok:
    __slots__ = ("sem", "val", "snap")

    def __init__(self, sem, val, snap=None):
        self.sem = sem
        self.val = val
        self.snap = snap


class Buf:
    __slots__ = ("name", "wtok", "rtoks", "dsem", "dcount", "excl")

    def __init__(self, name="", excl=False):
        self.excl = excl
        self.name = name
        self.wtok = None
        self.rtoks = {}
        self.dsem = None
        self.dcount = 0


class Eng:
    def __init__(self, nc, name, h):
        self.nc = nc
        self.name = name
        self.h = h
        self.sem = nc.alloc_semaphore(name="sem_" + name)
        self.count = 0
        self.seen = {}

    def wait(self, tok):
        if tok is None:
            return
        k = tok.sem.num
        if self.seen.get(k, 0) >= tok.val:
            return
        self.h.wait_ge(tok.sem, tok.val)
        NW[0] += 1
        self.seen[k] = tok.val
        if tok.snap:
            sn = self.seen
            for kk, vv in tok.snap.items():
                if sn.get(kk, 0) < vv:
                    sn[kk] = vv

    def deps(self, reads, writes):
        for b in reads:
            self.wait(b.wtok)
        for b in writes:
            self.wait(b.wtok)
            for t in b.rtoks.values():
                self.wait(t)

    def op(self, fn, reads=(), writes=()):
        ex = [b for b in reads if b.excl]
        if ex:
            reads = [b for b in reads if not b.excl]
            writes = list(writes) + ex
        self.deps(reads, writes)
        ins = fn()
        self.count += 1
        ins.then_inc(self.sem, 1)
        tok = Tok(self.sem, self.count, dict(self.seen))
        for b in reads:
            b.rtoks[self.name] = tok
        for b in writes:
            b.wtok = tok
            b.rtoks = {}
        return tok

    def dma(self, out, in_, reads, writes, cb):
        self.deps(reads, writes)
        if cb.dsem is None:
            cb.dsem = self.nc.alloc_semaphore(name="dsem_" + cb.name)
        self.h.dma_start(out=out, in_=in_).then_inc(cb.dsem, 16)
        cb.dcount += 16
        tok = Tok(cb.dsem, cb.dcount)
        for b in reads:
            b.rtoks["dma_" + cb.name] = tok
        for b in writes:
            b.wtok = tok
            b.rtoks = {}
        return tok


class Ring:
    def __init__(self, es, nc, name, shape, dtype, n, psum=False):
        self.items = []
        for i in range(n):
            alloc = nc.psum_tensor if psum else nc.sbuf_tensor
            t = es.enter_context(alloc(f"{name}{i}", shape, dtype))
            self.items.append((t, Buf(f"{name}{i}")))
        self.i = 0

    def next(self):
        r = self.items[self.i % len(self.items)]
        self.i += 1
        return r


def build_nc():
    nc = bass.Bass("TRN2", target_bir_lowering=False)
    es = ExitStack()

    def din(name, shape):
        return nc.dram_tensor(name, list(shape), F32, kind="ExternalInput").ap()

    def dout(name, shape):
        return nc.dram_tensor(name, list(shape), F32, kind="ExternalOutput").ap()

    x_in = [din(f"x{p}", (128, 16, NTMAX if p == 0 else PT)) for p in range(NPASS)]
    w_dr = [din(f"wt{l}", (NTILES, 128, 2048)) for l in range(L)]
    wba_dr = din("wba", (128, L * 16 * 16))
    cwa_dr = din("cwa", (128, L * 8 * 3))
    cwg_dr = din("cwg", (128, L * 24 * 4))
    nw_dr = din("nw", (128, L))
    lnp_dr = din("lnp", (128, 4 * L * 16))
    alog_dr = din("alog", (64, L * 8))
    dtb_dr = din("dtb", (64, L * 8))
    sca_dr = din("sca", (128, L * 8, 16 * 2))
    sgc_dr = din("sgc", (128, L * 24, 16 * 3))
    sg_dr = din("sg", (L, 8, 128, 16 * 128))
    cst_dr = din("cst", (128, 128 + 9 * 64 + 16 + 16 * 64))
    y_out = [dout(f"y{p}", (128, 16, NTMAX if p == 0 else PT)) for p in range(NPASS)]
    ocap_dr = dout("ocap", (128, L * 8 * 2))
    ogcp_dr = dout("ogcp", (128, L * 24 * 3))
    ogp_dr = dout("ogp", (128, L * 8 * 128))
    ocas_dr = dout("ocas", (128, L * 8, 16 * 2))
    ogcs_dr = dout("ogcs", (128, L * 24, 16 * 3))
    ogs_dr = dout("ogs", (L, 8, 128, 16 * 128))

    PE = Eng(nc, "pe", nc.tensor)
    V = Eng(nc, "dve", nc.vector)
    A = Eng(nc, "act", nc.scalar)
    G = Eng(nc, "pool", nc.gpsimd)
    SP = Eng(nc, "sp", nc.sync)

    def sb(name, shape, dt=F32):
        return es.enter_context(nc.sbuf_tensor(name, list(shape), dt))

    xT = sb("xT", (128, 16, NTMAX))
    xb = sb("xb", (128, 16, NTMAX), BF16)
    BxT = [Buf(f"xT{i}") for i in range(16)]
    Bxb = [Buf(f"xb{i}") for i in range(16)]
    Sp = sb("Sp", (128, L * 8, 128))
    BSp = [[Buf(f"Sp{l}_{h}") for h in range(8)] for l in range(L)]
    Spb = sb("Spb", (128, 8, 128), BF16)
    BSpb = [Buf(f"Spb{h}") for h in range(8)]
    ha = sb("ha", (128, L * 8, 2))
    hg = sb("hg", (128, L * 24, 3))
    Bha = [[Buf() for _ in range(8)] for l in range(L)]
    Bhg = [[Buf() for _ in range(24)] for l in range(L)]
    cst = sb("cst_sb", (128, 128 + 9 * 64 + 16 + 16 * 64))
    Bcst = Buf("cst")
    ident = cst[:, 0:128]
    I64 = cst[0:64, 0:64]
    o = 128
    Ltri = cst[0:64, o:o + 64]; o += 64
    Lblk = cst[0:64, o:o + 64]; o += 64
    ONES64 = cst[0:64, o:o + 64]; o += 64
    BLK = cst[0:64, o:o + 64]; o += 64
    negm_tri = cst[0:64, o:o + 64]; o += 64
    negm_blk = cst[0:64, o:o + 64]; o += 64
    sm_tri = cst[0:64, o:o + 64]; o += 64
    sm_blk = cst[0:64, o:o + 64]; o += 64
    o += 64
    rowind = cst[0:64, o:o + 16]; o += 16
    blockind = cst[:, o:o + 1024]; o += 1024
    ones128 = sb("ones128", (128, 128))
    ident_b = sb("ident_b", (128, 128), BF16)
    ONES128a = ones128[0:64, :]
    ones_b = sb("ones_b", (128, 128), BF16)
    sqb = [sb(f"sqb{i}", (128, NTMAX), BF16) for i in range(2)]
    Bsqb = [Buf(), Buf()]
    lnr_p = [sb(f"lnr_p{i}", (128, NTMAX)) for i in range(3)]
    Bones = Buf("ones")
    wba_f = sb("wba_fs", (128, L * 16 * 16))
    wba_b = sb("wba_b", (128, L * 16, 16), BF16)
    Bwba = Buf("wba")
    cwa = sb("cwa_sb", (128, L * 8, 3))
    cwg = sb("cwg_sb", (128, L * 24, 4))
    nw = sb("nw_sb", (128, L))
    lnp = sb("lnp_sb", (128, 4 * L, 16))
    Bpar = Buf("par")
    alog_bc = sb("alog_bc", (64, L * 8))
    dtb_bc = sb("dtb_bc", (64, L, 8))
    nA_bc = sb("nA_bc", (64, L, 8))
    wst = Ring(es, nc, "wst", (128, 1024), F32, 4)
    wbf = Ring(es, nc, "wbf", (128, 16, 128), BF16, 3)
    ps = [es.enter_context(nc.psum_tensor(f"ps{i}", [128, 512], F32)) for i in range(8)]
    Bps = [Buf(f"ps{i}", excl=True) for i in range(8)]

    scrA = Ring(es, nc, "scrA", (128, 640), F32, 4)
    colt = sb("colt", (64, 8, 9 * 8))
    Bcol = [Buf() for _ in range(8)]
    BSs = Buf("Ss")
    PH = {}
    Blnr_p = [Buf() for _ in range(3)]
    pending = []

    state = {"wt": 0}

    def wtile(l, key, nblk=16, pool_cast=False):
        t = TILE_ID[key]
        assert len(PLAN[t][1]) == nblk
        wb_, Bwb = wbf.next()
        hb = nblk // 2
        for hf in range(2):
            st, Bst = wst.next()
            n = hb * 128
            SP.dma(st[:, 0:n], w_dr[l][t, :, hf * n:(hf + 1) * n], [], [Bst], Bst)
            eng = A if (state["wt"] % 2 == 0) else V
            state["wt"] += 1
            src = st[:, 0:n].rearrange("p (k c) -> p k c", c=128)
            dst = wb_[:, hf * hb:(hf + 1) * hb, :]
            if pool_cast or eng is A:
                A.op(lambda: nc.scalar.activation(out=dst, in_=src, func=AF.Copy), [Bst], [Bwb])
            else:
                V.op(lambda: nc.vector.tensor_copy(out=dst, in_=src), [Bst], [Bwb])
        return wb_, Bwb

    class TS:
        def __init__(self, l, specs):
            self.l = l
            self.specs = specs
            self.i = 0
            self.pre = None

        def get(self, key):
            if self.pre is None:
                self.pre = wtile(self.l, *self.specs[0])
            assert self.specs[self.i][0] == key, (self.specs[self.i][0], key)
            cur = self.pre
            self.i += 1
            self.pre = wtile(self.l, *self.specs[self.i]) if self.i < len(self.specs) else None
            return cur

    def tblocks(nt):
        return [(0, PT)] + ([(PT, nt)] if nt > PT else [])

    def proj(wb_, Bwb, nk, rhs_fn, rhs_bufs, banks, nt):
        for bi, (t0, t1) in enumerate(tblocks(nt)):
            b = banks[bi]

            def fn(b=b, t0=t0, t1=t1):
                ins = None
                for k in range(nk):
                    ins = nc.tensor.matmul(ps[b][:, 0:t1 - t0], wb_[:, k, :], rhs_fn(k, t0, t1),
                                           start=(k == 0), stop=(k == nk - 1))
                return ins
            PE.op(fn, [Bwb] + rhs_bufs, [Bps[b]])

    def xb_rhs(k, t0, t1):
        return xb[:, k, t0:t1]

    pstate = {"sets": [(0, 1), (2, 3), (4, 5), (6, 7)], "i": 0}

    def pset():
        s = pstate["sets"][pstate["i"] % len(pstate["sets"])]
        pstate["i"] += 1
        return s

    def evac(eng, fn_mk, banks, nt, reads, writes):
        for bi, (t0, t1) in enumerate(tblocks(nt)):
            b = banks[bi]
            eng.op(lambda b=b, t0=t0, t1=t1: fn_mk(ps[b][:, 0:t1 - t0], t0, t1), [Bps[b]] + reads, writes)

    Bld = Buf("ld")
    SP.dma(cst[:], cst_dr, [], [Bcst], Bcst)
    SP.dma(wba_f[:], wba_dr, [], [Bwba], Bwba)
    SP.dma(cwa[:].rearrange("p a b -> p (a b)"), cwa_dr, [], [Bpar], Bld)
    SP.dma(cwg[:].rearrange("p a b -> p (a b)"), cwg_dr, [], [Bpar], Bld)
    SP.dma(nw[:], nw_dr, [], [Bpar], Bld)
    SP.dma(lnp[:].rearrange("p a b -> p (a b)"), lnp_dr, [], [Bpar], Bld)
    SP.dma(alog_bc[:], alog_dr, [], [Bpar], Bld)
    SP.dma(dtb_bc[:].rearrange("p a b -> p (a b)"), dtb_dr, [], [Bpar], Bld)
    G.op(lambda: nc.gpsimd.tensor_copy(out=wba_b[:].rearrange("p a b -> p (a b)"), in_=wba_f[:]), [Bwba], [Bwba])
    G.op(lambda: nc.gpsimd.memset(ones128[:], 1.0), [], [Bones])
    G.op(lambda: nc.gpsimd.tensor_copy(out=ident_b[:], in_=ident), [Bcst], [Bones])
    G.op(lambda: nc.gpsimd.memset(ones_b[:], 1.0), [], [Bones])
    G.op(lambda: nc.gpsimd.memset(Sp[:], 0.0), [], [b for r in BSp for b in r])
    G.op(lambda: nc.gpsimd.memset(ha[:], 0.0), [], [b for r in Bha for b in r])
    G.op(lambda: nc.gpsimd.memset(hg[:], 0.0), [], [b for r in Bhg for b in r])
    A.op(lambda: nc.scalar.activation(out=nA_bc[:].rearrange("p a b -> p (a b)"), in_=alog_bc[:], func=AF.Exp), [Bpar], [Bpar])
    V.op(lambda: nc.vector.tensor_scalar(out=nA_bc[:].rearrange("p a b -> p (a b)"), in0=nA_bc[:].rearrange("p a b -> p (a b)"),
                                          scalar1=-1.0, scalar2=None, op0=ALU.mult), [Bpar], [Bpar])

    def layer_norm(nt, g_col, b_col):
        blocks = tblocks(nt)
        for bi, (t0, t1) in enumerate(blocks):
            def fn(t0=t0, t1=t1, bi=bi):
                ins = None
                for kk in range(16):
                    ins = nc.tensor.matmul(ps[bi][:, 0:t1 - t0], ones128[:], xT[:, kk, t0:t1], start=(kk == 0), stop=(kk == 15))
                return ins
            PE.op(fn, [Bones] + BxT, [Bps[bi]])
        for k in range(16):
            s_, Bs_ = scrA.next()
            A.op(lambda s_=s_, k=k: nc.scalar.activation(out=s_[:, 0:nt], in_=xT[:, k, 0:nt], func=AF.Square),
                 [BxT[k]], [Bs_])
            for bi, (t0, t1) in enumerate(blocks):
                PE.op(lambda s_=s_, k=k, t0=t0, t1=t1, bi=bi: nc.tensor.matmul(ps[2 + bi][:, 0:t1 - t0], ones128[:], s_[:, t0:t1],
                                                                               start=(k == 0), stop=(k == 15), skip_group_check=True),
                      [Bones, Bs_], [Bps[2 + bi]])
        mean, msq, rstd = PH["lnr"]
        Blnr = PH["Blnr"]
        for bi, (t0, t1) in enumerate(blocks):
            n = t1 - t0
            A.op(lambda: nc.scalar.activation(out=mean[:, t0:t1], in_=ps[bi][:, 0:n], func=AF.Copy, scale=1.0 / D), [Bps[bi]], [Blnr[0]])
            V.op(lambda: nc.vector.tensor_tensor(out=msq[:, t0:t1], in0=mean[:, t0:t1], in1=mean[:, t0:t1], op=ALU.mult), [Blnr[0]], [Blnr[1]])
            V.op(lambda: nc.vector.scalar_tensor_tensor(out=rstd[:, t0:t1], in0=ps[2 + bi][:, 0:n], scalar=1.0 / D, in1=msq[:, t0:t1],
                                                        op0=ALU.mult, op1=ALU.subtract), [Bps[2 + bi], Blnr[1]], [Blnr[2]])
            A.op(lambda: nc.scalar.activation(out=rstd[:, t0:t1], in_=rstd[:, t0:t1], func=AF.Ln, bias=eps_ln[:, 0:1]), [Blnr[2], Bones], [Blnr[2]])
            A.op(lambda: nc.scalar.activation(out=rstd[:, t0:t1], in_=rstd[:, t0:t1], func=AF.Exp, scale=-0.5), [Blnr[2]], [Blnr[2]])
        for k in range(16):
            V.op(lambda k=k: nc.vector.tensor_tensor(out=xT[:, k, 0:nt], in0=xT[:, k, 0:nt], in1=mean[:, 0:nt], op=ALU.subtract),
                 [BxT[k], Blnr[0]], [BxT[k]])
            V.op(lambda k=k: nc.vector.tensor_tensor(out=xT[:, k, 0:nt], in0=xT[:, k, 0:nt], in1=rstd[:, 0:nt], op=ALU.mult),
                 [BxT[k], Blnr[2]], [BxT[k]])
            V.op(lambda k=k: nc.vector.tensor_scalar(out=xT[:, k, 0:nt], in0=xT[:, k, 0:nt], scalar1=g_col[:, k:k + 1], scalar2=b_col[:, k:k + 1],
                                                     op0=ALU.mult, op1=ALU.add), [BxT[k], Bpar], [BxT[k]])
            A.op(lambda k=k: nc.scalar.activation(out=xb[:, k, 0:nt], in_=xT[:, k, 0:nt], func=AF.Copy), [BxT[k]], [Bxb[k]])

    eps_ln = sb("eps_ln", (128, 3))
    G.op(lambda: nc.gpsimd.memset(eps_ln[:, 0:1], LN_EPS), [], [Bones])
    G.op(lambda: nc.gpsimd.memset(eps_ln[:, 1:2], NORM_EPS), [], [Bones])
    G.op(lambda: nc.gpsimd.memset(eps_ln[:, 2:3], 1.0), [], [Bones])

    def conv(pc, Bpc, K, wcol, out, Bout, nt, ns):
        H = K - 1
        A.op(lambda: nc.scalar.activation(out=out[:, 0:PT], in_=pc[:, 0:PT], func=AF.Copy, scale=wcol[:, 0:1]),
             [Bpc, Bpar], [Bout])
        for j in range(1, K):
            V.op(lambda j=j: nc.vector.scalar_tensor_tensor(out=out[:, 0:PT], in0=pc[:, j:j + PT], scalar=wcol[:, j:j + 1], in1=out[:, 0:PT],
                                                            op0=ALU.mult, op1=ALU.add), [Bpc, Bpar, Bout], [Bout])
        if ns:
            pcs = pc[:, H + PT:H + PT + 16 * (H + 4)].rearrange("p (s w) -> p s w", w=H + 4)
            outs = out[:, PT:PT + NS].rearrange("p (s t) -> p s t", t=4)
            A.op(lambda: nc.scalar.activation(out=outs, in_=pcs[:, :, 0:4], func=AF.Copy, scale=wcol[:, 0:1]),
                 [Bpc, Bpar], [Bout])
            for j in range(1, K):
                V.op(lambda j=j: nc.vector.scalar_tensor_tensor(out=outs, in0=pcs[:, :, j:j + 4], scalar=wcol[:, j:j + 1], in1=outs,
                                                                op0=ALU.mult, op1=ALU.add), [Bpc, Bpar, Bout], [Bout])

    def fill_pc(pc, Bpc, K, banks, nt, ns, halo_ap, Bhalo, st_dr, ost_dr, via=None):
        H = K - 1
        G.op(lambda: nc.gpsimd.tensor_copy(out=pc[:, 0:H], in_=halo_ap), [Bhalo], [Bpc])
        if via is None:
            A.op(lambda: nc.scalar.activation(out=pc[:, H:H + PT], in_=ps[banks[0]][:, 0:PT], func=AF.Copy), [Bps[banks[0]]], [Bpc])
        else:
            via(pc[:, H:H + PT], 0)
        G.op(lambda: nc.gpsimd.tensor_copy(out=halo_ap, in_=pc[:, PT:PT + H]), [Bpc], [Bhalo])
        if ns:
            pcs = pc[:, H + PT:H + PT + 16 * (H + 4)].rearrange("p (s w) -> p s w", w=H + 4)
            pending.append(SP.dma(pcs[:, :, 0:H], st_dr.rearrange("p (s w) -> p s w", w=H), [], [Bpc], Bpc))
            if via is None:
                A.op(lambda: nc.scalar.activation(out=pcs[:, :, H:H + 4], in_=ps[banks[1]][:, 0:NS].rearrange("p (s t) -> p s t", t=4),
                                                  func=AF.Copy), [Bps[banks[1]]], [Bpc])
            else:
                via(pcs[:, :, H:H + 4], 1)
            pending.append(SP.dma(ost_dr.rearrange("p (s w) -> p s w", w=H), pcs[:, :, 4:4 + H], [Bpc], [], Bpc))

    def barrier():
        engs = (PE, V, A, G, SP)
        toks = [Tok(e.sem, e.count) for e in engs if e.count > 0]
        for e in engs:
            for t in toks:
                if t.sem is not e.sem:
                    e.wait(t)
            for t in pending:
                e.wait(t)
        del pending[:]

    gst = {"i": 0, "GB": (4, 5, 6), "ob": 0}

    def gbank():
        GB = gst["GB"]
        b = GB[gst["i"] % len(GB)]
        gst["i"] += 1
        return b


    def gdn_cols(l, nt, ns):
        nch = 8 + (1 if ns else 0)
        w = nch * 8
        b = gbank()

        def fn():
            ins = None
            for c in range(nch):
                for k in range(16):
                    ins = nc.tensor.matmul(ps[b][0:64, c * 16:(c + 1) * 16], xb[:, k, c * 64:(c + 1) * 64], wba_b[:, l * 16 + k, :],
                                           start=(k == 0), stop=(k == 15))
            return ins
        PE.op(fn, Bxb + [Bwba], [Bps[b]])
        pv = ps[b][0:64, 0:nch * 16].rearrange("p (c x) -> p c x", x=16)

        def ct(i):
            return colt[:, i, 0:w]

        def ct3(i):
            return colt[:, i, 0:w].rearrange("p (c h) -> p c h", h=8)
        A.op(lambda: nc.scalar.activation(out=ct3(0), in_=pv[:, :, 0:8], func=AF.Sigmoid), [Bps[b]], [Bcol[0]])
        V.op(lambda: nc.vector.tensor_tensor(out=ct3(5), in0=pv[:, :, 8:16], in1=dtb_bc[:, l:l + 1, :].to_broadcast([64, nch, 8]), op=ALU.add),
             [Bps[b], Bpar], [Bcol[5]])
        G.op(lambda: nc.gpsimd.tensor_scalar(out=ct(6), in0=ct(5), scalar1=-1.0, scalar2=None, op0=ALU.mult), [Bcol[5]], [Bcol[6]])
        V.op(lambda: nc.vector.tensor_tensor(out=ct(6), in0=ct(6), in1=ct(5), op=ALU.min), [Bcol[5], Bcol[6]], [Bcol[6]])
        A.op(lambda: nc.scalar.activation(out=ct(6), in_=ct(6), func=AF.Exp), [Bcol[6]], [Bcol[6]])
        A.op(lambda: nc.scalar.activation(out=ct(6), in_=ct(6), func=AF.Ln, bias=eps_ln[0:64, 2:3]), [Bcol[6], Bones], [Bcol[6]])
        V.op(lambda: nc.vector.scalar_tensor_tensor(out=ct(5), in0=ct(5), scalar=0.0, in1=ct(6), op0=ALU.max, op1=ALU.add),
             [Bcol[5], Bcol[6]], [Bcol[5]])
        V.op(lambda: nc.vector.tensor_tensor(out=ct3(1), in0=ct3(5), in1=nA_bc[:, l:l + 1, :].to_broadcast([64, nch, 8]), op=ALU.mult),
             [Bcol[5], Bpar], [Bcol[1]])
        b2 = gbank()

        def fn2():
            nc.tensor.matmul(ps[b2][0:64, 0:64], Ltri, colt[:, 1, 0:64], start=True, stop=True)
            ins = nc.tensor.matmul(ps[b2][0:64, 128:192], ONES64, colt[:, 1, 0:64], start=True, stop=True, skip_group_check=True)
            if ns:
                nc.tensor.matmul(ps[b2][0:64, 64:72], Lblk, colt[:, 1, 64:72], start=True, stop=True, skip_group_check=True)
                ins = nc.tensor.matmul(ps[b2][0:64, 192:200], BLK, colt[:, 1, 64:72], start=True, stop=True, skip_group_check=True)
            return ins
        PE.op(fn2, [Bcol[1], Bcst], [Bps[b2]])
        A.op(lambda: nc.scalar.activation(out=ct(2), in_=ps[b2][0:64, 0:w], func=AF.Copy), [Bps[b2]], [Bcol[2]])
        V.op(lambda: nc.vector.tensor_tensor(out=ct(3), in0=ps[b2][0:64, 128:128 + w], in1=ct(2), op=ALU.subtract), [Bps[b2], Bcol[2]], [Bcol[3]])
        A.op(lambda: nc.scalar.activation(out=ct(3), in_=ct(3), func=AF.Exp), [Bcol[3]], [Bcol[3]])
        A.op(lambda: nc.scalar.activation(out=ct(4), in_=ct(2), func=AF.Exp), [Bcol[2]], [Bcol[4]])
        V.op(lambda: nc.vector.tensor_tensor(out=ct(4), in0=ct(4), in1=ct(0), op=ALU.mult), [Bcol[4], Bcol[0]], [Bcol[4]])

    def gdn_batch(l, h, c0, nb, t0, sample, R):
        qnb, knb, vsb = R["qkv"]
        Bqnb, Bknb, Bv = R["Bqkv"]
        Bq, Bk = Bqnb, Bknb
        qn, kn = qnb, knb
        szt, Bsz, ogT, BogT = R["szt"], R["Bsz"], PH["ogT"], PH["BogT"]
        BK = R["BK"]
        OB = BK[1]
        bst = {"scan": False}

        def gbank():
            if bst["scan"]:
                return BK[0]
            R["bi"] += 1
            return BK[R["bi"] % 2]

        def nb_(name):
            return R["gn"][name]
        W = nb * 64
        T1 = t0 + W
        Lm = Lblk if sample else Ltri
        negm = negm_blk if sample else negm_tri
        smk = sm_blk if sample else sm_tri

        def col(i):
            return colt[:, i, 0:72].rearrange("p (c h) -> p c h", h=8)[:, c0:c0 + nb, h]

        def v3(ap, inner):
            return ap.rearrange("p (c x) -> p c x", x=inner)

        Gbc, BG = nb_("Gbc")
        Bbc, BB = nb_("Bbc")
        A.op(lambda: nc.scalar.activation(out=v3(Gbc[0:64, 0:W], 64), in_=Lm.unsqueeze(1).to_broadcast([64, nb, 64]), func=AF.Copy), [Bcst], [BG])
        V.op(lambda: nc.vector.tensor_tensor(out=v3(Gbc[0:64, 0:W], 64), in0=v3(Gbc[0:64, 0:W], 64),
                                             in1=col(1).unsqueeze(2).to_broadcast([64, nb, 64]), op=ALU.mult), [BG, Bcol[1]], [BG])
        V.op(lambda: nc.vector.tensor_tensor(out=v3(Bbc[0:64, 0:W], 64), in0=I64.unsqueeze(1).to_broadcast([64, nb, 64]),
                                             in1=col(0).unsqueeze(2).to_broadcast([64, nb, 64]), op=ALU.mult), [Bcst, Bcol[0]], [BB])
        br = gbank()

        def fn():
            nc.tensor.matmul(ps[br][:, 0:W], ONES128a, Gbc[0:64, 0:W], start=True, stop=True, skip_group_check=True)
            return nc.tensor.matmul(ps[br][:, 256:256 + W], ONES128a, Bbc[0:64, 0:W], start=True, stop=True, skip_group_check=True)
        PE.op(fn, [BG, BB, Bcst, Bones], [Bps[br]])
        yield
        bkt = gbank()

        def fn6():
            ins = None
            for ci in range(nb):
                tk0 = t0 + ci * 64
                ins = nc.tensor.matmul(ps[bkt][0:64, ci * 128:(ci + 1) * 128], knb[:, tk0:tk0 + 64], ident_b[:], start=True, stop=True,
                                       skip_group_check=True)
            return ins
        PE.op(fn6, [Bk, Bones], [Bps[bkt]])
        yield
        egrow, Beg = nb_("egrow")
        kbT, Bkb = nb_("kbT")
        qdT, Bqd = nb_("qdT")
        tt, Btt = nb_("tt")
        decT, Bdec = nb_("decT")
        decTs, Bdecs = nb_("decTs")
        A.op(lambda: nc.scalar.activation(out=egrow[:, 0:W], in_=ps[br][:, 0:W], func=AF.Exp), [Bps[br]], [Beg])
        V.op(lambda: nc.vector.tensor_tensor(out=kbT[:, 0:W], in0=ps[br][:, 256:256 + W], in1=kn[:, t0:T1], op=ALU.mult), [Bps[br], Bk], [Bkb])
        V.op(lambda: nc.vector.tensor_tensor(out=v3(tt[0:64, 0:W], 64), in0=v3(ps[br][0:64, 0:W], 64),
                                             in1=col(2).unsqueeze(2).to_broadcast([64, nb, 64]), op=ALU.subtract), [Bps[br], Bcol[2]], [Btt])
        V.op(lambda: nc.vector.scalar_tensor_tensor(out=v3(tt[0:64, 0:W], 64), in0=v3(tt[0:64, 0:W], 64), scalar=0.0,
                                                    in1=negm.unsqueeze(1).to_broadcast([64, nb, 64]), op0=ALU.min, op1=ALU.add), [Btt, Bcst], [Btt])
        yield
        A.op(lambda: nc.scalar.activation(out=decT[0:64, 0:W], in_=tt[0:64, 0:W], func=AF.Exp), [Btt], [Bdec])
        V.op(lambda: nc.vector.tensor_tensor(out=qdT[:, 0:W], in0=qn[:, t0:T1], in1=egrow[:, 0:W], op=ALU.mult), [Bq, Beg], [Bqd])
        yield
        V.op(lambda: nc.vector.tensor_tensor(out=v3(decTs[0:64, 0:W], 64), in0=v3(decT[0:64, 0:W], 64),
                                             in1=smk.unsqueeze(1).to_broadcast([64, nb, 64]), op=ALU.mult), [Bdec, Bcst], [Bdecs])
        kbg, Bkbg = nb_("kbg")
        kd, Bkd = nb_("kd")
        vb, Bvb = nb_("vb")
        V.op(lambda: nc.vector.tensor_tensor(out=v3(kbg[0:64, 0:nb * 128], 128), in0=v3(ps[bkt][0:64, 0:nb * 128], 128),
                                             in1=col(4).unsqueeze(2).to_broadcast([64, nb, 128]), op=ALU.mult), [Bps[bkt], Bcol[4]], [Bkbg])
        V.op(lambda: nc.vector.tensor_tensor(out=v3(kd[0:64, 0:nb * 128], 128), in0=v3(ps[bkt][0:64, 0:nb * 128], 128),
                                             in1=col(3).unsqueeze(2).to_broadcast([64, nb, 128]), op=ALU.mult), [Bps[bkt], Bcol[3]], [Bkd])
        yield
        bk = gbank()

        def fn4():
            ins = None
            for ci in range(nb):
                tk0 = t0 + ci * 64
                nc.tensor.matmul(ps[bk][0:64, ci * 64:(ci + 1) * 64], knb[:, tk0:tk0 + 64], kbT[:, ci * 64:(ci + 1) * 64], start=True, stop=True,
                                 skip_group_check=True)
                ins = nc.tensor.matmul(ps[bk][0:64, 256 + ci * 64:256 + (ci + 1) * 64], knb[:, tk0:tk0 + 64], qnb[:, tk0:tk0 + 64], start=True, stop=True,
                                       skip_group_check=True)
            return ins
        PE.op(fn4, [Bknb, Bqnb, Bkb], [Bps[bk]])
        yield
        bvt = gbank()

        def fn6b():
            ins = None
            for ci in range(nb):
                tk0 = t0 + ci * 64
                ins = nc.tensor.matmul(ps[bvt][0:64, ci * 128:(ci + 1) * 128], vsb[:, tk0:tk0 + 64], ident_b[:], start=True, stop=True,
                                       skip_group_check=True)
            return ins
        PE.op(fn6b, [Bv, Bones], [Bps[bvt]])
        yield
        MT, BMT = nb_("MT")
        MTb, BMTb = nb_("MTb")
        AT, BAT = nb_("AT")
        V.op(lambda: nc.vector.tensor_tensor(out=MT[0:64, 0:W], in0=ps[bk][0:64, 0:W], in1=decTs[0:64, 0:W], op=ALU.mult), [Bps[bk], Bdecs], [BMT])
        V.op(lambda: nc.vector.tensor_tensor(out=MTb[0:64, 0:W], in0=ps[bk][0:64, 0:W], in1=decTs[0:64, 0:W], op=ALU.mult), [Bps[bk], Bdecs], [BMTb])
        V.op(lambda: nc.vector.tensor_tensor(out=AT[0:64, 0:W], in0=ps[bk][0:64, 256:256 + W], in1=decT[0:64, 0:W], op=ALU.mult), [Bps[bk], Bdec], [BAT])
        V.op(lambda: nc.vector.tensor_tensor(out=v3(vb[0:64, 0:nb * 128], 128), in0=v3(ps[bvt][0:64, 0:nb * 128], 128),
                                             in1=col(0).unsqueeze(2).to_broadcast([64, nb, 128]), op=ALU.mult), [Bps[bvt], Bcol[0]], [Bvb])
        yield
        bm = gbank()

        def fn5():
            ins = None
            for ci in range(nb):
                ins = nc.tensor.transpose(ps[bm][0:64, ci * 64:(ci + 1) * 64], MT[0:64, ci * 64:(ci + 1) * 64], I64)
            return ins
        PE.op(fn5, [BMT, Bcst], [Bps[bm]])
        yield
        Mb, BMb = nb_("Mb")
        P32, BP32 = nb_("P32")
        Pb, BPb = nb_("Pb")
        TTb, BTT = nb_("MTb")
        A.op(lambda: nc.scalar.activation(out=Mb[0:64, 0:W], in_=ps[bm][0:64, 0:W], func=AF.Copy), [Bps[bm]], [BMb])
        V.op(lambda: nc.vector.scalar_tensor_tensor(out=v3(Pb[0:64, 0:W], 64), in0=v3(MT[0:64, 0:W], 64), scalar=-1.0,
                                                    in1=I64.unsqueeze(1).to_broadcast([64, nb, 64]), op0=ALU.mult, op1=ALU.add), [BMT, Bcst], [BPb])
        V.op(lambda: nc.vector.scalar_tensor_tensor(out=v3(P32[0:64, 0:W], 64), in0=v3(MT[0:64, 0:W], 64), scalar=-1.0,
                                                    in1=I64.unsqueeze(1).to_broadcast([64, nb, 64]), op0=ALU.mult, op1=ALU.add), [BMT, Bcst], [BP32])
        nst = 1 if sample else 5

        def sq(lhs_a, rhs_a, Bl, Br):
            b_ = gbank()

            def f():
                ins = None
                for ci in range(nb):
                    sl = slice(ci * 64, (ci + 1) * 64)
                    ins = nc.tensor.matmul(ps[b_][0:64, sl], lhs_a[0:64, sl], rhs_a[0:64, sl], start=True, stop=True, skip_group_check=True)
                return ins
            PE.op(f, [Bl, Br], [Bps[b_]])
            return b_
        yield
        b1 = sq(Mb, MTb, BMb, BMTb)
        yield
        b2_ = sq(MTb, Mb, BMTb, BMb)
        yield
        Acur, BAc = nb_("A0")
        Atc, BAtc = nb_("At0")
        A.op(lambda: nc.scalar.activation(out=Acur[0:64, 0:W], in_=ps[b1][0:64, 0:W], func=AF.Copy), [Bps[b1]], [BAc])
        V.op(lambda: nc.vector.tensor_copy(out=Atc[0:64, 0:W], in_=ps[b2_][0:64, 0:W]), [Bps[b2_]], [BAtc])
        yield
        for n in range(1, nst + 1):
            bp = sq(Atc, Pb, BAtc, BPb)
            yield
            if n == nst:
                V.op(lambda bp=bp: nc.vector.tensor_tensor(out=TTb[0:64, 0:W], in0=ps[bp][0:64, 0:W], in1=P32[0:64, 0:W], op=ALU.add), [Bps[bp], BP32], [BTT])
            else:
                V.op(lambda bp=bp: nc.vector.tensor_tensor(out=Pb[0:64, 0:W], in0=ps[bp][0:64, 0:W], in1=P32[0:64, 0:W], op=ALU.add), [Bps[bp], BP32], [BPb])
                V.op(lambda bp=bp: nc.vector.tensor_tensor(out=P32[0:64, 0:W], in0=ps[bp][0:64, 0:W], in1=P32[0:64, 0:W], op=ALU.add), [Bps[bp], BP32], [BP32])
                ba = sq(Atc, Acur, BAtc, BAc)
                yield
                bat = sq(Acur, Atc, BAc, BAtc)
                yield
                An, BAn = nb_("A1" if n % 2 == 1 else "A0")
                Atn, BAtn = nb_("At1" if n % 2 == 1 else "At0")
                A.op(lambda ba=ba, An=An: nc.scalar.activation(out=An[0:64, 0:W], in_=ps[ba][0:64, 0:W], func=AF.Copy), [Bps[ba]], [BAn])
                V.op(lambda bat=bat, Atn=Atn: nc.vector.tensor_copy(out=Atn[0:64, 0:W], in_=ps[bat][0:64, 0:W]), [Bps[bat]], [BAtn])
                Acur, BAc, Atc, BAtc = An, BAn, Atn, BAtn
            yield
        bw = gbank()

        def fn7():
            ins = None
            for ci in range(nb):
                ins = nc.tensor.matmul(ps[bw][:, ci * 64:(ci + 1) * 64], kbg[0:64, ci * 128:(ci + 1) * 128], TTb[0:64, ci * 64:(ci + 1) * 64],
                                       start=True, stop=True, skip_group_check=True)
            return ins
        PE.op(fn7, [Bkbg, BTT], [Bps[bw]])
        yield
        wT, BwT = nb_("wT")
        A.op(lambda: nc.scalar.activation(out=wT[:, 0:W], in_=ps[bw][:, 0:W], func=AF.Copy), [Bps[bw]], [BwT])
        bu = gbank()

        def fn7b():
            ins = None
            for ci in range(nb):
                ins = nc.tensor.matmul(ps[bu][0:64, ci * 128:(ci + 1) * 128], TTb[0:64, ci * 64:(ci + 1) * 64], vb[0:64, ci * 128:(ci + 1) * 128],
                                       start=True, stop=True, skip_group_check=True)
            return ins
        PE.op(fn7b, [BTT, Bvb], [Bps[bu]])
        yield
        u, Bu = nb_("Gbc")
        A.op(lambda: nc.scalar.activation(out=u[0:64, 0:nb * 128], in_=ps[bu][0:64, 0:nb * 128], func=AF.Copy), [Bps[bu]], [Bu])
        yield
        bst["scan"] = True
        if not sample:
            S = Sp[:, l * 8 + h, :]
            BS = BSp[l][h]
            Sb = Spb[:, h, :]
            BSb = BSpb[h]
            for ci in range(nb):
                bws = gbank()
                PE.op(lambda: nc.tensor.matmul(ps[bws][0:64, 0:128], wT[:, ci * 64:(ci + 1) * 64], Sb, start=True, stop=True), [BwT, BSb], [Bps[bws]])
                yield
                vn, Bvn = nb_("A0" if ci % 2 == 0 else "At0")
                V.op(lambda: nc.vector.scalar_tensor_tensor(out=vn[0:64, 0:128], in0=ps[bws][0:64, 0:128], scalar=-1.0,
                                                            in1=u[0:64, ci * 128:(ci + 1) * 128], op0=ALU.mult, op1=ALU.add), [Bps[bws], Bu], [Bvn])
                yield

                def fo():
                    nc.tensor.matmul(ps[OB][0:64, ci * 128:(ci + 1) * 128], qdT[:, ci * 64:(ci + 1) * 64], Sb, start=True, stop=False,
                                     skip_group_check=True)
                    return nc.tensor.matmul(ps[OB][0:64, ci * 128:(ci + 1) * 128], AT[0:64, ci * 64:(ci + 1) * 64], vn[0:64, 0:128],
                                            start=False, stop=True, skip_group_check=True)
                PE.op(fo, [Bqd, BSb, BAT, Bvn], [Bps[OB]])
                yield
                bs_ = gbank()
                PE.op(lambda: nc.tensor.matmul(ps[bs_][:, 0:128], kd[0:64, ci * 128:(ci + 1) * 128], vn[0:64, 0:128], start=True, stop=True),
                      [Bkd, Bvn], [Bps[bs_]])
                yield
                V.op(lambda: nc.vector.scalar_tensor_tensor(out=Sb, in0=S, scalar=egrow[:, ci * 64 + 63:ci * 64 + 64], in1=ps[bs_][:, 0:128],
                                                            op0=ALU.mult, op1=ALU.add), [BS, Beg, Bps[bs_]], [BSb])
                V.op(lambda: nc.vector.scalar_tensor_tensor(out=S, in0=S, scalar=egrow[:, ci * 64 + 63:ci * 64 + 64], in1=ps[bs_][:, 0:128],
                                                            op0=ALU.mult, op1=ALU.add), [BS, Beg, Bps[bs_]], [BS])
                yield
        else:
            Ss, Ssb = PH["Ss"], PH["Ssb"]
            BSsb = PH["BSsb"]
            pending.append(SP.dma(Ss[:].rearrange("p s v -> p (s v)"), sg_dr[l, h], [], [BSs], BSs))
            A.op(lambda: nc.scalar.activation(out=Ssb[:], in_=Ss[:], func=AF.Copy), [BSs], [BSsb])
            wTm, qdm, kdm = PH["wTm"], PH["qdm"], PH["kdm"]
            Bmsk = [PH["BwTm"], PH["Bkdm"]]
            Bmsk3 = PH["Bqdm"]
            G.op(lambda: nc.gpsimd.tensor_tensor(out=wTm[:, :, 0:64], in0=wT[:, 0:64].unsqueeze(1).to_broadcast([128, 16, 64]),
                                                 in1=blockind.rearrange("p (s i) -> p s i", i=64), op=ALU.mult), [BwT, Bcst], [Bmsk[0]])
            G.op(lambda: nc.gpsimd.tensor_tensor(out=qdm[:], in0=qdT[:, 0:64].unsqueeze(1).to_broadcast([128, 16, 64]),
                                                 in1=blockind.rearrange("p (s i) -> p s i", i=64), op=ALU.mult), [Bqd, Bcst], [Bmsk3])
            G.op(lambda: nc.gpsimd.tensor_tensor(out=kdm[0:64, :, :], in0=kd[0:64, 0:128].unsqueeze(1).to_broadcast([64, 16, 128]),
                                                 in1=rowind.unsqueeze(2).to_broadcast([64, 16, 128]), op=ALU.mult), [Bkd, Bcst], [Bmsk[1]])
            bws = gbank()

            def fws():
                ins = None
                for s in range(16):
                    ins = nc.tensor.matmul(ps[bws][0:64, 0:128], wTm[:, s, 0:64], Ssb[:, s, :], start=(s == 0), stop=(s == 15))
                return ins
            PE.op(fws, [Bmsk[0], BSsb], [Bps[bws]])
            yield
            vn, Bvn = nb_("A0")
            V.op(lambda: nc.vector.scalar_tensor_tensor(out=vn[0:64, 0:128], in0=ps[bws][0:64, 0:128], scalar=-1.0, in1=u[0:64, 0:128],
                                                        op0=ALU.mult, op1=ALU.add), [Bps[bws], Bu], [Bvn])

            def fo():
                for s in range(16):
                    nc.tensor.matmul(ps[OB][0:64, 0:128], qdm[:, s, :], Ssb[:, s, :], start=(s == 0), stop=False)
                return nc.tensor.matmul(ps[OB][0:64, 0:128], AT[0:64, 0:64], vn[0:64, 0:128], start=False, stop=True)
            PE.op(fo, [Bmsk3, BSsb, BAT, Bvn], [Bps[OB]])
            yield
            egl = egrow[:, 0:64].rearrange("p (s t) -> p s t", t=4)[:, :, 3:4]
            for g4 in range(4):
                bs_ = gbank()

                def fs(g4=g4, bs_=bs_):
                    ins = None
                    for j in range(4):
                        s = g4 * 4 + j
                        ins = nc.tensor.matmul(ps[bs_][:, j * 128:(j + 1) * 128], kdm[0:64, s, :], vn[0:64, 0:128], start=True, stop=True,
                                               skip_group_check=True)
                    return ins
                PE.op(fs, [Bmsk[1], Bvn], [Bps[bs_]])
                yield
                if g4 == 0:
                    G.op(lambda: nc.gpsimd.tensor_tensor(out=Ss[:], in0=Ss[:], in1=egl.to_broadcast([128, 16, 128]), op=ALU.mult), [BSs, Beg], [BSs])
                V.op(lambda g4=g4, bs_=bs_: nc.vector.tensor_tensor(out=Ss[:, g4 * 4:(g4 + 1) * 4, :], in0=ps[bs_][:, 0:512].rearrange("p (s v) -> p s v", v=128),
                                                                    in1=Ss[:, g4 * 4:(g4 + 1) * 4, :], op=ALU.add), [Bps[bs_], BSs], [BSs])
            pending.append(SP.dma(ogs_dr[l, h], Ss[:].rearrange("p s v -> p (s v)"), [BSs], [], BSs))
        yield
        osq, Bosq = nb_("Gbc")
        A.op(lambda: nc.scalar.activation(out=osq[0:64, 0:nb * 128], in_=ps[OB][0:64, 0:nb * 128], func=AF.Square), [Bps[OB]], [Bosq])
        yield
        rs, Brs = nb_("rs")
        V.op(lambda: nc.vector.tensor_reduce(out=rs[0:64, 0:nb], in_=v3(osq[0:64, 0:nb * 128], 128), axis=AX.X, op=ALU.add), [Bosq], [Brs])
        yield
        A.op(lambda: nc.scalar.activation(out=rs[0:64, 0:nb], in_=rs[0:64, 0:nb], func=AF.Ln, scale=1.0 / 128, bias=eps_ln[0:64, 1:2]), [Brs, Bones], [Brs])
        A.op(lambda: nc.scalar.activation(out=rs[0:64, 0:nb], in_=rs[0:64, 0:nb], func=AF.Exp, scale=-0.5), [Brs], [Brs])
        yield
        on, Bon = nb_("Bbc")
        V.op(lambda: nc.vector.tensor_tensor(out=v3(on[0:64, 0:nb * 128], 128), in0=v3(ps[OB][0:64, 0:nb * 128], 128),
                                             in1=rs[0:64, 0:nb].unsqueeze(2).to_broadcast([64, nb, 128]), op=ALU.mult), [Bps[OB], Brs], [Bon])
        bot = gbank()

        def fot():
            ins = None
            for ci in range(nb):
                ins = nc.tensor.transpose(ps[bot][:, ci * 64:(ci + 1) * 64], on[0:64, ci * 128:(ci + 1) * 128], I64)
            return ins
        PE.op(fot, [Bon, Bcst], [Bps[bot]])
        yield
        V.op(lambda: nc.vector.scalar_tensor_tensor(out=ogT[:, h, t0:T1], in0=ps[bot][:, 0:W], scalar=nw[:, l:l + 1], in1=szt[:, t0:T1],
                                                    op0=ALU.mult, op1=ALU.add if False else ALU.mult), [Bps[bot], Bpar, Bsz], [BogT[h]])

    def layer(l, p):
        ns = NS if p == 0 else 0
        nt = PT + ns
        with ExitStack() as mix:
            def sbm(stack, name, shape, dt=F32):
                return stack.enter_context(nc.sbuf_tensor(f"{name}_{l}_{p}", list(shape), dt))
            ogT = sbm(mix, "ogT", (128, 8, NTMAX), BF16)
            BogT = [Buf() for _ in range(8)]
            K = 2 if ns else 3
            GN = {"Gbc": (512, F32), "Bbc": (512, F32), "egrow": (256, F32), "kbT": (256, BF16), "qdT": (256, BF16), "tt": (256, F32),
                  "decT": (256, F32), "decTs": (256, F32), "MT": (256, F32), "MTb": (256, BF16), "Mb": (256, BF16), "AT": (256, BF16),
                  "P32": (256, F32), "Pb": (256, BF16), "A0": (256, BF16), "At0": (256, BF16), "A1": (256, BF16),
                  "At1": (256, BF16), "kbg": (512, BF16), "kd": (512, BF16), "vb": (512, BF16), "wT": (256, BF16), "rs": (16, F32)}
            with ExitStack() as gs:
                PH["ogT"], PH["BogT"] = ogT, BogT
                slots = []
                for k in range(K):
                    R = {"qkv": [sbm(gs, f"qkv{i}_{k}", (128, NTMAX), BF16) for i in range(3)], "Bqkv": [Buf() for _ in range(3)],
                         "szt": sbm(gs, f"szt{k}", (128, NTMAX), BF16), "Bsz": Buf(),
                         "gn": {nm: (sbm(gs, f"gn{nm}_{k}", (128, sz), dt_), Buf()) for nm, (sz, dt_) in GN.items()},
                         "BK": (2 + k, 5 + k), "bi": 0}
                    slots.append(R)
                if ns:
                    PH["Ss"] = sbm(gs, "Ss", (128, 16, 128))
                    PH["Ssb"] = sbm(gs, "Ssb", (128, 16, 128), BF16); PH["BSsb"] = Buf()
                    PH["wTm"] = sbm(gs, "wTm", (128, 16, 64), BF16); PH["BwTm"] = Buf()
                    PH["qdm"] = sbm(gs, "qdm", (128, 16, 64), BF16); PH["Bqdm"] = Buf()
                    PH["kdm"] = sbm(gs, "kdm", (128, 16, 128), BF16); PH["Bkdm"] = Buf()
                    pstate["sets"] = [(0, 1)]
                    gst["GB"] = (2, 3, 4, 5)
                else:
                    pstate["sets"] = [(0,), (1,)]
                    gst["GB"] = (2, 3, 4)
                ckpt("start")
                gdn_cols(l, nt, ns)
                ckpt("cols")

                def head_gen(h, R):
                    qkv, Bqkv, szt, Bsz = R["qkv"], R["Bqkv"], R["szt"], R["Bsz"]
                    for xi, (fin, Bfin_) in enumerate(zip(qkv, Bqkv)):
                        wb_, Bwb = wtile(l, ("w_in", (O_Q, O_K, O_V)[xi] // 128 + h, 0), pool_cast=True)
                        bk_ = pset()
                        proj(wb_, Bwb, 16, xb_rhs, Bxb, bk_, nt)
                        ch = xi * 8 + h
                        pc, Bpc = scrA.next()
                        dst, Bd = scrA.next()
                        fill_pc(pc, Bpc, 4, bk_, nt, ns, hg[:, l * 24 + ch, :], Bhg[l][ch], sgc_dr[:, l * 24 + ch, :], ogcs_dr[:, l * 24 + ch, :])
                        conv(pc, Bpc, 4, cwg[:, l * 24 + ch, :], dst, Bd, nt, ns)
                        if xi == 2:
                            A.op(lambda dst=dst, fin=fin: nc.scalar.activation(out=fin[:, 0:nt], in_=dst[:, 0:nt], func=AF.Silu), [Bd], [Bfin_])
                        else:
                            A.op(lambda dst=dst: nc.scalar.activation(out=dst[:, 0:nt], in_=dst[:, 0:nt], func=AF.Silu), [Bd], [Bd])
                            sq_, Bsq = scrA.next()
                            A.op(lambda dst=dst, sq_=sq_: nc.scalar.activation(out=sq_[:, 0:nt], in_=dst[:, 0:nt], func=AF.Square), [Bd], [Bsq])
                            bk2 = pset()
                            for bi, (t0, t1) in enumerate(tblocks(nt)):
                                PE.op(lambda bi=bi, t0=t0, t1=t1, sq_=sq_, bk2=bk2: nc.tensor.matmul(ps[bk2[bi]][:, 0:t1 - t0], ones128[:], sq_[:, t0:t1], start=True, stop=True),
                                      [Bones, Bsq], [Bps[bk2[bi]]])
                            rr, Brr = scrA.next()
                            evac(A, lambda pa, t0, t1: nc.scalar.activation(out=rr[:, t0:t1], in_=pa, func=AF.Ln, bias=eps_ln[:, 1:2]), bk2, nt, [Bones], [Brr])
                            A.op(lambda rr=rr: nc.scalar.activation(out=rr[:, 0:nt], in_=rr[:, 0:nt], func=AF.Exp, scale=-0.5), [Brr], [Brr])
                            if xi == 0:
                                V.op(lambda dst=dst, rr=rr, fin=fin: nc.vector.scalar_tensor_tensor(out=fin[:, 0:nt], in0=dst[:, 0:nt], scalar=float(128 ** -0.5),
                                                                                                    in1=rr[:, 0:nt], op0=ALU.mult, op1=ALU.mult), [Bd, Brr], [Bfin_])
                            else:
                                V.op(lambda dst=dst, rr=rr, fin=fin: nc.vector.tensor_tensor(out=fin[:, 0:nt], in0=dst[:, 0:nt], in1=rr[:, 0:nt], op=ALU.mult), [Bd, Brr], [Bfin_])
                        yield (8 if xi == 2 else 12)
                    A.op(lambda: nc.scalar.activation(out=Spb[:, h, :], in_=Sp[:, l * 8 + h, :], func=AF.Copy), [BSp[l][h]], [BSpb[h]])
                    wb_, Bwb = wtile(l, ("w_in", O_Z // 128 + h, 0), pool_cast=True)
                    bk_ = pset()
                    proj(wb_, Bwb, 16, xb_rhs, Bxb, bk_, nt)
                    evac(A, lambda pa, t0, t1: nc.scalar.activation(out=szt[:, t0:t1], in_=pa, func=AF.Silu), bk_, nt, [], [Bsz])
                    yield 4
                    yield from gdn_batch(l, h, 0, 4, 0, False, R)
                    yield from gdn_batch(l, h, 4, 4, 256, False, R)
                    if ns:
                        for _ in gdn_batch(l, h, 8, 1, 512, True, R):
                            pass

                def head_gen_pair(h, R):
                    qkv, Bqkv, szt, Bsz = R["qkv"], R["Bqkv"], R["szt"], R["Bsz"]
                    banks = [(0, 1), (4, 7)] if ns else [(0,), (1,)]
                    tiles = [wtile(l, ("w_in", (O_Q, O_K)[xi] // 128 + h, 0), pool_cast=True) for xi in range(2)]
                    for xi in range(2):
                        proj(tiles[xi][0], tiles[xi][1], 16, xb_rhs, Bxb, banks[xi], nt)
                    pcs_, dsts_ = [], []
                    for xi in range(2):
                        ch = xi * 8 + h
                        pc, Bpc = scrA.items[2 * xi]
                        dst, Bd = scrA.items[2 * xi + 1]
                        pcs_.append((pc, Bpc)); dsts_.append((dst, Bd))
                        fill_pc(pc, Bpc, 4, banks[xi], nt, ns, hg[:, l * 24 + ch, :], Bhg[l][ch], sgc_dr[:, l * 24 + ch, :], ogcs_dr[:, l * 24 + ch, :])
                    for xi in range(2):
                        ch = xi * 8 + h
                        conv(pcs_[xi][0], pcs_[xi][1], 4, cwg[:, l * 24 + ch, :], dsts_[xi][0], dsts_[xi][1], nt, ns)
                    for xi in range(2):
                        dst, Bd = dsts_[xi]
                        A.op(lambda dst=dst: nc.scalar.activation(out=dst[:, 0:nt], in_=dst[:, 0:nt], func=AF.Silu), [Bd], [Bd])
                    for xi in range(2):
                        dst, Bd = dsts_[xi]
                        A.op(lambda dst=dst, xi=xi: nc.scalar.activation(out=sqb[xi][:, 0:nt], in_=dst[:, 0:nt], func=AF.Square), [Bd], [Bsqb[xi]])
                    for xi in range(2):
                        for bi, (t0, t1) in enumerate(tblocks(nt)):
                            PE.op(lambda xi=xi, bi=bi, t0=t0, t1=t1: nc.tensor.matmul(ps[banks[xi][bi]][:, 0:t1 - t0], ones_b[:], sqb[xi][:, t0:t1], start=True, stop=True),
                                  [Bones, Bsqb[xi]], [Bps[banks[xi][bi]]])
                    for xi in range(2):
                        rr, Brr = pcs_[xi]
                        evac(A, lambda pa, t0, t1, rr=rr: nc.scalar.activation(out=rr[:, t0:t1], in_=pa, func=AF.Ln, bias=eps_ln[:, 1:2]), banks[xi], nt, [Bones], [Brr])
                    for xi in range(2):
                        rr, Brr = pcs_[xi]
                        A.op(lambda rr=rr: nc.scalar.activation(out=rr[:, 0:nt], in_=rr[:, 0:nt], func=AF.Exp, scale=-0.5), [Brr], [Brr])
                    for xi in range(2):
                        rr, Brr = pcs_[xi]
                        dst, Bd = dsts_[xi]
                        fin, Bfin_ = qkv[xi], Bqkv[xi]
                        if xi == 0:
                            V.op(lambda: nc.vector.scalar_tensor_tensor(out=fin[:, 0:nt], in0=dst[:, 0:nt], scalar=float(128 ** -0.5),
                                                                        in1=rr[:, 0:nt], op0=ALU.mult, op1=ALU.mult), [Bd, Brr], [Bfin_])
                        else:
                            V.op(lambda: nc.vector.tensor_tensor(out=fin[:, 0:nt], in0=dst[:, 0:nt], in1=rr[:, 0:nt], op=ALU.mult), [Bd, Brr], [Bfin_])
                    yield 20
                    tv = wtile(l, ("w_in", O_V // 128 + h, 0), pool_cast=True)
                    tz = wtile(l, ("w_in", O_Z // 128 + h, 0), pool_cast=True)
                    proj(tv[0], tv[1], 16, xb_rhs, Bxb, banks[0], nt)
                    proj(tz[0], tz[1], 16, xb_rhs, Bxb, banks[1], nt)
                    ch = 2 * 8 + h
                    pc, Bpc = scrA.items[0]
                    dst, Bd = scrA.items[1]
                    fill_pc(pc, Bpc, 4, banks[0], nt, ns, hg[:, l * 24 + ch, :], Bhg[l][ch], sgc_dr[:, l * 24 + ch, :], ogcs_dr[:, l * 24 + ch, :])
                    evac(A, lambda pa, t0, t1: nc.scalar.activation(out=szt[:, t0:t1], in_=pa, func=AF.Silu), banks[1], nt, [], [Bsz])
                    conv(pc, Bpc, 4, cwg[:, l * 24 + ch, :], dst, Bd, nt, ns)
                    A.op(lambda: nc.scalar.activation(out=qkv[2][:, 0:nt], in_=dst[:, 0:nt], func=AF.Silu), [Bd], [Bqkv[2]])
                    A.op(lambda: nc.scalar.activation(out=Spb[:, h, :], in_=Sp[:, l * 8 + h, :], func=AF.Copy), [BSp[l][h]], [BSpb[h]])
                    yield 12
                    yield from gdn_batch(l, h, 0, 4, 0, False, R)
                    yield from gdn_batch(l, h, 4, 4, 256, False, R)
                    if ns:
                        for _ in gdn_batch(l, h, 8, 1, 512, True, R):
                            pass

                active = []
                free = [(k, k * (104 // K)) for k in range(K)]
                nxt = 0
                while nxt < 8 or active:
                    while free and nxt < 8:
                        k, vt0 = free.pop(0)
                        active.append([(head_gen if ns else head_gen_pair)(nxt, slots[k]), k, vt0])
                        nxt += 1
                    item = min(active, key=lambda it: it[2])
                    try:
                        w = next(item[0])
                        item[2] += (w or 1)
                    except StopIteration:
                        active.remove(item)
                        free.append((item[1], item[2]))
                barrier()
            ckpt("gdn")
            a_in = sbm(mix, "a_in", (128, 8, NTMAX), BF16)
            Ba_in = [Buf() for _ in range(8)]
            specs = []
            for cc_ in range(8):
                specs += [(("w_in", O_GB // 128 + cc_, 0), 16), (("w_in", O_GC // 128 + cc_, 0), 16), (("w_in", O_H // 128 + cc_, 0), 16)]
            for oc_ in range(16):
                specs += [(("w_a_out", oc_, 0), 8), (("w_b_out", oc_, 0), 8), (("w_in_ma", oc_, 0), 16), (("w_in_mb", oc_, 0), 16)]
            for oc_ in range(16):
                specs.append((("w_o", oc_, 0), 16))
            for fc_ in range(64):
                specs.append((("w_up", fc_, 0), 16))
            for oc_ in range(16):
                for g_ in range(4):
                    specs.append((("w_down", oc_, g_), 16))
            ts = TS(l, specs)
            PH["ts"] = ts
            pstate["sets"] = [(0, 1), (2, 3), (4, 5), (6, 7)]
            for cc in range(8):
                wgB, BwgB = ts.get(("w_in", O_GB // 128 + cc, 0))
                bB = pset()
                proj(wgB, BwgB, 16, xb_rhs, Bxb, bB, nt)
                wgC, BwgC = ts.get(("w_in", O_GC // 128 + cc, 0))
                bC = pset()
                proj(wgC, BwgC, 16, xb_rhs, Bxb, bC, nt)
                wh, Bwh = ts.get(("w_in", O_H // 128 + cc, 0))
                bH = pset()
                proj(wh, Bwh, 16, xb_rhs, Bxb, bH, nt)
                hsb, Bhsb = scrA.next()
                evac(A, lambda pa, t0, t1: nc.scalar.activation(out=hsb[:, t0:t1], in_=pa, func=AF.Copy), bH, nt, [], [Bhsb])
                pc, Bpc = scrA.next()

                def via(dst_ap, which, bC=bC, hsb=hsb, Bhsb=Bhsb, Bpc=Bpc):
                    if which == 0:
                        V.op(lambda: nc.vector.tensor_tensor(out=dst_ap, in0=ps[bC[0]][:, 0:PT], in1=hsb[:, 0:PT], op=ALU.mult), [Bps[bC[0]], Bhsb], [Bpc])
                    else:
                        V.op(lambda: nc.vector.tensor_tensor(out=dst_ap, in0=ps[bC[1]][:, 0:NS].rearrange("p (s t) -> p s t", t=4),
                                                             in1=hsb[:, PT:PT + NS].rearrange("p (s t) -> p s t", t=4), op=ALU.mult), [Bps[bC[1]], Bhsb], [Bpc])
                fill_pc(pc, Bpc, 3, None, nt, ns, ha[:, l * 8 + cc, :], Bha[l][cc], sca_dr[:, l * 8 + cc, :], ocas_dr[:, l * 8 + cc, :], via=via)
                cva, Bcva = scrA.next()
                conv(pc, Bpc, 3, cwa[:, l * 8 + cc, :], cva, Bcva, nt, ns)
                evac(V, lambda pa, t0, t1: nc.vector.tensor_tensor(out=a_in[:, cc, t0:t1], in0=pa, in1=cva[:, t0:t1], op=ALU.mult), bB, nt, [Bcva], [Ba_in[cc]])
            ckpt("mixA")
            with ExitStack() as mg:
                big = sbm(mg, "mrg", (128, 16, NTMAX), BF16)
                Bbig = [Buf() for _ in range(16)]
                for oc in range(16):
                    wa, Bwa = ts.get(("w_a_out", oc, 0))
                    bya = pset()
                    proj(wa, Bwa, 8, lambda k, t0, t1: a_in[:, k, t0:t1], Ba_in, bya, nt)
                    wbo, Bwbo = ts.get(("w_b_out", oc, 0))
                    byb = pset()
                    proj(wbo, Bwbo, 8, lambda k, t0, t1: ogT[:, k, t0:t1], BogT, byb, nt)
                    wma, Bwma = ts.get(("w_in_ma", oc, 0))
                    bma = pset()
                    proj(wma, Bwma, 16, xb_rhs, Bxb, bma, nt)
                    wmb, Bwmb = ts.get(("w_in_mb", oc, 0))
                    bmb = pset()
                    proj(wmb, Bwmb, 16, xb_rhs, Bxb, bmb, nt)
                    sa, Bsa = scrA.next()
                    sbb, Bsbb = scrA.next()
                    evac(A, lambda pa, t0, t1: nc.scalar.activation(out=sa[:, t0:t1], in_=pa, func=AF.Sigmoid), bma, nt, [], [Bsa])
                    evac(A, lambda pa, t0, t1: nc.scalar.activation(out=sbb[:, t0:t1], in_=pa, func=AF.Sigmoid), bmb, nt, [], [Bsbb])
                    evac(V, lambda pa, t0, t1: nc.vector.tensor_tensor(out=sa[:, t0:t1], in0=pa, in1=sa[:, t0:t1], op=ALU.mult), bya, nt, [Bsa], [Bsa])
                    evac(V, lambda pa, t0, t1: nc.vector.tensor_tensor(out=sbb[:, t0:t1], in0=pa, in1=sbb[:, t0:t1], op=ALU.mult), byb, nt, [Bsbb], [Bsbb])
                    V.op(lambda: nc.vector.tensor_tensor(out=big[:, oc, 0:nt], in0=sa[:, 0:nt], in1=sbb[:, 0:nt], op=ALU.add), [Bsa, Bsbb], [Bbig[oc]])
                for oc in range(16):
                    wo, Bwo = ts.get(("w_o", oc, 0))
                    bo = pset()
                    proj(wo, Bwo, 16, lambda k, t0, t1: big[:, k, t0:t1], Bbig, bo, nt)
                    evac(V, lambda pa, t0, t1: nc.vector.scalar_tensor_tensor(out=xT[:, oc, t0:t1], in0=xT[:, oc, t0:t1], scalar=ALPHA, in1=pa,
                                                                              op0=ALU.mult, op1=ALU.add), bo, nt, [BxT[oc]], [BxT[oc]])
        ckpt("wo")
        PH["lnr"] = lnr_p
        PH["Blnr"] = Blnr_p
        layer_norm(nt, lnp[:, (0 * L + l), :], lnp[:, (1 * L + l), :])
        barrier()
        ckpt("ln1")
        with ExitStack() as mlp:
            big = mlp.enter_context(nc.sbuf_tensor(f"hid_{l}_{p}", [128, 64, NTMAX], BF16))
            Bbig = [Buf() for _ in range(64)]
            for fc in range(64):
                wu, Bwu = PH["ts"].get(("w_up", fc, 0))
                bu_ = pset()
                proj(wu, Bwu, 16, xb_rhs, Bxb, bu_, nt)
                r_, Br_ = scrA.next()
                evac(A, lambda pa, t0, t1: nc.scalar.activation(out=r_[:, t0:t1], in_=pa, func=AF.Relu), bu_, nt, [], [Br_])
                V.op(lambda: nc.vector.tensor_tensor(out=big[:, fc, 0:nt], in0=r_[:, 0:nt], in1=r_[:, 0:nt], op=ALU.mult), [Br_], [Bbig[fc]])
            for oc in range(16):
                bd_ = pset()
                for g in range(4):
                    wb_, Bwb = PH["ts"].get(("w_down", oc, g))
                    for bi, (t0, t1) in enumerate(tblocks(nt)):
                        def fn(wb_=wb_, g=g, t0=t0, t1=t1, b=bd_[bi]):
                            ins = None
                            for k in range(16):
                                ins = nc.tensor.matmul(ps[b][:, 0:t1 - t0], wb_[:, k, :], big[:, g * 16 + k, t0:t1],
                                                       start=(g == 0 and k == 0), stop=(g == 3 and k == 15), skip_group_check=True)
                            return ins
                        PE.op(fn, [Bwb] + Bbig[g * 16:(g + 1) * 16], [Bps[bd_[bi]]])
                evac(V, lambda pa, t0, t1: nc.vector.scalar_tensor_tensor(out=xT[:, oc, t0:t1], in0=xT[:, oc, t0:t1], scalar=ALPHA, in1=pa,
                                                                          op0=ALU.mult, op1=ALU.add), bd_, nt, [BxT[oc]], [BxT[oc]])
        layer_norm(nt, lnp[:, (2 * L + l), :], lnp[:, (3 * L + l), :])
        barrier()
        ckpt("layer")

    Bxio = Buf("xio")
    stopped = False
    for p in range(NPASS):
        if stopped:
            break
        nt = NTMAX if p == 0 else PT
        SP.dma(xT[:, :, 0:nt], x_in[p], [], BxT, Bxio)
        for k in range(16):
            A.op(lambda k=k: nc.scalar.activation(out=xb[:, k, 0:nt], in_=xT[:, k, 0:nt], func=AF.Copy), [BxT[k]], [Bxb[k]])
        try:
            for l in range(L):
                layer(l, p)
        except StopBuild:
            stopped = True
        pending.append(SP.dma(y_out[p], xT[:, :, 0:nt], BxT, [], Bxio))
    Bfin = Buf("fin")
    if DBG.get("skip_fin"):
        SP.dma(ogp_dr, Sp[:].rearrange("p a b -> p (a b)"), [b for r in BSp for b in r], [], Bfin)
        SP.wait(Tok(Bfin.dsem, Bfin.dcount))
        SP.wait(Tok(Bxio.dsem, Bxio.dcount))
        return nc
    SP.dma(ocap_dr, ha[:].rearrange("p a b -> p (a b)"), [b for r in Bha for b in r], [], Bfin)
    SP.dma(ogcp_dr, hg[:].rearrange("p a b -> p (a b)"), [b for r in Bhg for b in r], [], Bfin)
    SP.dma(ogp_dr, Sp[:].rearrange("p a b -> p (a b)"), [b for r in BSp for b in r], [], Bfin)
    SP.wait(Tok(Bfin.dsem, Bfin.dcount))
    SP.wait(Tok(Bxio.dsem, Bxio.dcount))
    if BSs.dsem is not None:
        SP.wait(Tok(BSs.dsem, BSs.dcount))
    for t in pending:
        SP.wait(t)
    for (_, bb) in scrA.items:
        if bb.dsem is not None:
            SP.wait(Tok(bb.dsem, bb.dcount))
    return nc


def _fm(a):
    t = a.shape[0]
    return np.ascontiguousarray(a.T.reshape(16, 128, t).transpose(1, 0, 2))


def _consts():
    c = np.zeros((128, 128 + 9 * 64 + 16 + 16 * 64), np.float32)
    c[:, 0:128] = np.eye(128, dtype=np.float32)
    j = np.arange(64)[:, None]
    i = np.arange(64)[None, :]
    same = (j // 4) == (i // 4)
    o = 128
    mats = [(j <= i), same & (j <= i), np.ones((64, 64), bool), same]
    for m in mats:
        c[0:64, o:o + 64] = m.astype(np.float32); o += 64
    c[0:64, o:o + 64] = np.where(i >= j, 0.0, NEG); o += 64
    c[0:64, o:o + 64] = np.where(same & (i >= j), 0.0, NEG); o += 64
    c[0:64, o:o + 64] = (i > j).astype(np.float32); o += 64
    c[0:64, o:o + 64] = (same & (i > j)).astype(np.float32); o += 64
    c[:, o:o + 64] = 1.0; o += 64
    c[0:64, o:o + 16] = ((np.arange(64)[:, None] // 4) == np.arange(16)[None, :]).astype(np.float32); o += 16
    bi = ((np.arange(64)[None, :] // 4) == np.arange(16)[:, None]).astype(np.float32)
    c[:, o:o + 1024] = bi.reshape(1, 1024)
    return c


def _chunkvec(v, nch):
    sh = v.shape[:-1]
    return np.moveaxis(v.reshape(sh + (nch, 128)), -1, 0)


_NC_CACHE = {}


def kernel(x_prompt, x_sample, state_conv_a, state_gdn_conv, state_gdn, w_in, conv_a_w, gdn_conv_w, a_log, dt_bias,
           gdn_norm_w, w_a_out, w_b_out, w_o, ln1_g, ln1_b, w_up, w_down, ln2_g, ln2_b):
    f = lambda a: np.asarray(a, dtype=np.float32)
    x_prompt, x_sample, state_conv_a, state_gdn_conv, state_gdn = map(f, (x_prompt, x_sample, state_conv_a, state_gdn_conv, state_gdn))
    w_in, w_a_out, w_b_out, w_o, w_up, w_down = map(f, (w_in, w_a_out, w_b_out, w_o, w_up, w_down))
    wts = []
    for l in range(L):
        mats = {"w_in": w_in[l], "w_a_out": w_a_out[l], "w_b_out": w_b_out[l], "w_o": w_o[l], "w_up": w_up[l], "w_down": w_down[l],
                "w_in_ma": w_in[l][:, O_MA:O_MA + 2048], "w_in_mb": w_in[l][:, O_MB:O_MB + 2048]}
        arr = np.zeros((NTILES, 128, 2048), np.float32)
        for t, (mn, blks) in enumerate(PLAN):
            M = mats[mn]
            for bi, (rc, cc) in enumerate(blks):
                arr[t, :, bi * 128:(bi + 1) * 128] = M[rc * 128:(rc + 1) * 128, cc * 128:(cc + 1) * 128]
        wts.append(arr)
    wba = np.stack([w_in[l][:, O_B:O_B + 16].reshape(16, 128, 16).transpose(1, 0, 2) for l in range(L)], axis=1)
    wba = np.ascontiguousarray(wba).reshape(128, -1)
    cwa = np.ascontiguousarray(_chunkvec(f(conv_a_w), 8).transpose(0, 1, 3, 2)).reshape(128, -1)
    cwg = np.ascontiguousarray(_chunkvec(f(gdn_conv_w), 24).transpose(0, 1, 3, 2)).reshape(128, -1)
    nw = np.ascontiguousarray(f(gdn_norm_w).T)
    lnp = np.ascontiguousarray(np.stack([_chunkvec(f(v), 16) for v in (ln1_g, ln1_b, ln2_g, ln2_b)], axis=1)).reshape(128, -1)
    alog = np.ascontiguousarray(np.broadcast_to(f(a_log).reshape(1, -1), (64, L * 8)))
    dtb = np.ascontiguousarray(np.broadcast_to(f(dt_bias).reshape(1, -1), (64, L * 8)))
    cst = _consts()
    in_maps = []
    for c in range(NCORES):
        s = c % 4
        sl = slice(16 * c, 16 * c + 16)
        m = {}
        xs = x_sample[sl].reshape(64, D)
        for p in range(NPASS):
            xp = x_prompt[s, p * PT:(p + 1) * PT]
            if p == 0:
                xp = np.concatenate([xp, xs], axis=0)
            m[f"x{p}"] = _fm(xp)
        for l in range(L):
            m[f"wt{l}"] = wts[l]
        m["wba"] = wba; m["cwa"] = cwa; m["cwg"] = cwg; m["nw"] = nw; m["lnp"] = lnp; m["alog"] = alog; m["dtb"] = dtb; m["cst"] = cst
        sca = state_conv_a[:, sl].reshape(L, 16, 2, 8, 128).transpose(4, 0, 3, 1, 2)
        m["sca"] = np.ascontiguousarray(sca).reshape(128, -1)
        sgc = state_gdn_conv[:, sl].reshape(L, 16, 3, 24, 128).transpose(4, 0, 3, 1, 2)
        m["sgc"] = np.ascontiguousarray(sgc).reshape(128, -1)
        sg = state_gdn[:, sl].transpose(0, 2, 3, 1, 4)
        m["sg"] = np.ascontiguousarray(sg).reshape(L, 8, 128, 16 * 128)
        in_maps.append(m)
    if "nc" not in _NC_CACHE:
        _NC_CACHE["nc"] = build_nc()
    nc = _NC_CACHE["nc"]
    res = run_bass_kernel_spmd(nc, in_maps, core_ids=list(range(NCORES)))
    R = res.results
    y_prompt = np.zeros((4, 2048, D), np.float32)
    y_sample = np.zeros((128, 4, D), np.float32)
    nca_p = np.zeros((L, 4, 2, 1024), np.float32)
    ngc_p = np.zeros((L, 4, 3, 3072), np.float32)
    ng_p = np.zeros((L, 4, 8, 128, 128), np.float32)
    nca_s = np.zeros((L, 128, 2, 1024), np.float32)
    ngc_s = np.zeros((L, 128, 3, 3072), np.float32)
    ng_s = np.zeros((L, 128, 8, 128, 128), np.float32)

    def unfm(a):
        return a.transpose(2, 1, 0).reshape(a.shape[2], D)
    for c in range(NCORES):
        r = R[c]
        sl = slice(16 * c, 16 * c + 16)
        if c < 4:
            for p in range(NPASS):
                y_prompt[c, p * PT:(p + 1) * PT] = unfm(np.asarray(r[f"y{p}"])[:, :, 0:PT])
            nca_p[:, c] = np.asarray(r["ocap"]).reshape(128, L, 8, 2).transpose(1, 3, 2, 0).reshape(L, 2, 1024)
            ngc_p[:, c] = np.asarray(r["ogcp"]).reshape(128, L, 24, 3).transpose(1, 3, 2, 0).reshape(L, 3, 3072)
            ng_p[:, c] = np.asarray(r["ogp"]).reshape(128, L, 8, 128).transpose(1, 2, 0, 3)
        y_sample[sl] = unfm(np.asarray(r["y0"])[:, :, PT:PT + NS]).reshape(16, 4, D)
        nca_s[:, sl] = np.asarray(r["ocas"]).reshape(128, L, 8, 16, 2).transpose(1, 3, 4, 2, 0).reshape(L, 16, 2, 1024)
        ngc_s[:, sl] = np.asarray(r["ogcs"]).reshape(128, L, 24, 16, 3).transpose(1, 3, 4, 2, 0).reshape(L, 16, 3, 3072)
        ng_s[:, sl] = np.asarray(r["ogs"]).reshape(L, 8, 128, 16, 128).transpose(0, 3, 1, 2, 4)
    return (y_prompt, y_sample, nca_p, ngc_p, ng_p, nca_s, ngc_s, ng_s)
```

```python
import numpy as np
from contextlib import ExitStack
import concourse.bass as bass
import concourse.mybir as mybir
from concourse.bass_utils import run_bass_kernel_spmd

F32 = mybir.dt.float32
BF16 = mybir.dt.bfloat16
AF = mybir.ActivationFunctionType
ALU = mybir.AluOpType
AX = mybir.AxisListType

L = 2
D = 2048
NCORES = 8
ALPHA = float((2 * L) ** 0.25)
LN_EPS = 1e-5
NORM_EPS = 1e-6
NPASS = 4
PT = 512
NS = 64
NTMAX = PT + NS
NEG = -30000.0
O_GB, O_GC, O_H, O_Q, O_K, O_V, O_Z, O_B, O_A, O_MA, O_MB = 0, 1024, 2048, 3072, 4096, 5120, 6144, 7168, 7176, 7184, 9232


def tile_plan():
    tiles = []
    for h in range(8):
        for off in (O_Q, O_K, O_V, O_Z):
            tiles.append(("w_in", [(kc, (off // 128) + h) for kc in range(16)]))
    for cc in range(8):
        for off in (O_GB, O_GC, O_H):
            tiles.append(("w_in", [(kc, (off // 128) + cc) for kc in range(16)]))
    for oc in range(16):
        tiles.append(("w_a_out", [(kc, oc) for kc in range(8)]))
        tiles.append(("w_b_out", [(kc, oc) for kc in range(8)]))
        tiles.append(("w_in_ma", [(kc, oc) for kc in range(16)]))
        tiles.append(("w_in_mb", [(kc, oc) for kc in range(16)]))
    for oc in range(16):
        tiles.append(("w_o", [(kc, oc) for kc in range(16)]))
    for fc in range(64):
        tiles.append(("w_up", [(kc, fc) for kc in range(16)]))
    for oc in range(16):
        for g in range(4):
            tiles.append(("w_down", [(g * 16 + kc, oc) for kc in range(16)]))
    return tiles


PLAN = tile_plan()
NTILES = len(PLAN)
TILE_ID = {}
for _i, (_m, _b) in enumerate(PLAN):
    TILE_ID[(_m, _b[0][1], _b[0][0] // 16)] = _i


DBG = {"stop": None}
NW = [0]


class StopBuild(Exception):
    pass


def ckpt(name):
    if DBG["stop"] == name:
        raise StopBuild(name)


class Tok:
    __slots__ = ("sem", "val", "snap")

    def __init__(self, sem, val, snap=None):
        self.sem = sem
        self.val = val
        self.snap = snap


class Buf:
    __slots__ = ("name", "wtok", "rtoks", "dsem", "dcount", "excl")

    def __init__(self, name="", excl=False):
        self.excl = excl
        self.name = name
        self.wtok = None
        self.rtoks = {}
        self.dsem = None
        self.dcount = 0


class Eng:
    def __init__(self, nc, name, h):
        self.nc = nc
        self.name = name
        self.h = h
        self.sem = nc.alloc_semaphore(name="sem_" + name)
        self.count = 0
        self.seen = {}

    def wait(self, tok):
        if tok is None:
            return
        k = tok.sem.num
        if self.seen.get(k, 0) >= tok.val:
            return
        self.h.wait_ge(tok.sem, tok.val)
        NW[0] += 1
        self.seen[k] = tok.val
        if tok.snap:
            sn = self.seen
            for kk, vv in tok.snap.items():
                if sn.get(kk, 0) < vv:
                    sn[kk] = vv

    def deps(self, reads, writes):
        for b in reads:
            self.wait(b.wtok)
        for b in writes:
            self.wait(b.wtok)
            for t in b.rtoks.values():
                self.wait(t)

    def op(self, fn, reads=(), writes=()):
        ex = [b for b in reads if b.excl]
        if ex:
            reads = [b for b in reads if not b.excl]
            writes = list(writes) + ex
        self.deps(reads, writes)
        ins = fn()
        self.count += 1
        ins.then_inc(self.sem, 1)
        tok = Tok(self.sem, self.count, dict(self.seen))
        for b in reads:
            b.rtoks[self.name] = tok
        for b in writes:
            b.wtok = tok
            b.rtoks = {}
        return tok

    def dma(self, out, in_, reads, writes, cb):
        self.deps(reads, writes)
        if cb.dsem is None:
            cb.dsem = self.nc.alloc_semaphore(name="dsem_" + cb.name)
        self.h.dma_start(out=out, in_=in_).then_inc(cb.dsem, 16)
        cb.dcount += 16
        tok = Tok(cb.dsem, cb.dcount)
        for b in reads:
            b.rtoks["dma_" + cb.name] = tok
        for b in writes:
            b.wtok = tok
            b.rtoks = {}
        return tok


class Ring:
    def __init__(self, es, nc, name, shape, dtype, n, psum=False):
        self.items = []
        for i in range(n):
            alloc = nc.psum_tensor if psum else nc.sbuf_tensor
            t = es.enter_context(alloc(f"{name}{i}", shape, dtype))
            self.items.append((t, Buf(f"{name}{i}")))
        self.i = 0

    def next(self):
        r = self.items[self.i % len(self.items)]
        self.i += 1
        return r


def build_nc():
    nc = bass.Bass("TRN2", target_bir_lowering=False)
    es = ExitStack()

    def din(name, shape):
        return nc.dram_tensor(name, list(shape), F32, kind="ExternalInput").ap()

    def dout(name, shape):
        return nc.dram_tensor(name, list(shape), F32, kind="ExternalOutput").ap()

    x_in = [din(f"x{p}", (128, 16, NTMAX if p == 0 else PT)) for p in range(NPASS)]
    w_dr = [din(f"wt{l}", (NTILES, 128, 2048)) for l in range(L)]
    wba_dr = din("wba", (128, L * 16 * 16))
    cwa_dr = din("cwa", (128, L * 8 * 3))
    cwg_dr = din("cwg", (128, L * 24 * 4))
    nw_dr = din("nw", (128, L))
    lnp_dr = din("lnp", (128, 4 * L * 16))
    alog_dr = din("alog", (64, L * 8))
    dtb_dr = din("dtb", (64, L * 8))
    sca_dr = din("sca", (128, L * 8, 16 * 2))
    sgc_dr = din("sgc", (128, L * 24, 16 * 3))
    sg_dr = din("sg", (L, 8, 128, 16 * 128))
    cst_dr = din("cst", (128, 128 + 9 * 64 + 16 + 16 * 64))
    y_out = [dout(f"y{p}", (128, 16, NTMAX if p == 0 else PT)) for p in range(NPASS)]
    ocap_dr = dout("ocap", (128, L * 8 * 2))
    ogcp_dr = dout("ogcp", (128, L * 24 * 3))
    ogp_dr = dout("ogp", (128, L * 8 * 128))
    ocas_dr = dout("ocas", (128, L * 8, 16 * 2))
    ogcs_dr = dout("ogcs", (128, L * 24, 16 * 3))
    ogs_dr = dout("ogs", (L, 8, 128, 16 * 128))

    PE = Eng(nc, "pe", nc.tensor)
    V = Eng(nc, "dve", nc.vector)
    A = Eng(nc, "act", nc.scalar)
    G = Eng(nc, "pool", nc.gpsimd)
    SP = Eng(nc, "sp", nc.sync)

    def sb(name, shape, dt=F32):
        return es.enter_context(nc.sbuf_tensor(name, list(shape), dt))

    xT = sb("xT", (128, 16, NTMAX))
    xb = sb("xb", (128, 16, NTMAX), BF16)
    BxT = [Buf(f"xT{i}") for i in range(16)]
    Bxb = [Buf(f"xb{i}") for i in range(16)]
    Sp = sb("Sp", (128, L * 8, 128))
    BSp = [[Buf(f"Sp{l}_{h}") for h in range(8)] for l in range(L)]
    Spb = sb("Spb", (128, 8, 128), BF16)
    BSpb = [Buf(f"Spb{h}") for h in range(8)]
    ha = sb("ha", (128, L * 8, 2))
    hg = sb("hg", (128, L * 24, 3))
    Bha = [[Buf() for _ in range(8)] for l in range(L)]
    Bhg = [[Buf() for _ in range(24)] for l in range(L)]
    cst = sb("cst_sb", (128, 128 + 9 * 64 + 16 + 16 * 64))
    Bcst = Buf("cst")
    ident = cst[:, 0:128]
    I64 = cst[0:64, 0:64]
    o = 128
    Ltri = cst[0:64, o:o + 64]; o += 64
    Lblk = cst[0:64, o:o + 64]; o += 64
    ONES64 = cst[0:64, o:o + 64]; o += 64
    BLK = cst[0:64, o:o + 64]; o += 64
    negm_tri = cst[0:64, o:o + 64]; o += 64
    negm_blk = cst[0:64, o:o + 64]; o += 64
    sm_tri = cst[0:64, o:o + 64]; o += 64
    sm_blk = cst[0:64, o:o + 64]; o += 64
    o += 64
    rowind = cst[0:64, o:o + 16]; o += 16
    blockind = cst[:, o:o + 1024]; o += 1024
    ones128 = sb("ones128", (128, 128))
    ident_b = sb("ident_b", (128, 128), BF16)
    ONES128a = ones128[0:64, :]
    ones_b = sb("ones_b", (128, 128), BF16)
    sqb = [sb(f"sqb{i}", (128, NTMAX), BF16) for i in range(2)]
    Bsqb = [Buf(), Buf()]
    lnr_p = [sb(f"lnr_p{i}", (128, NTMAX)) for i in range(3)]
    Bones = Buf("ones")
    wba_f = sb("wba_fs", (128, L * 16 * 16))
    wba_b = sb("wba_b", (128, L * 16, 16), BF16)
    Bwba = Buf("wba")
    cwa = sb("cwa_sb", (128, L * 8, 3))
    cwg = sb("cwg_sb", (128, L * 24, 4))
    nw = sb("nw_sb", (128, L))
    lnp = sb("lnp_sb", (128, 4 * L, 16))
    Bpar = Buf("par")
    alog_bc = sb("alog_bc", (64, L * 8))
    dtb_bc = sb("dtb_bc", (64, L, 8))
    nA_bc = sb("nA_bc", (64, L, 8))
    wst = Ring(es, nc, "wst", (128, 1024), F32, 4)
    wbf = Ring(es, nc, "wbf", (128, 16, 128), BF16, 3)
    ps = [es.enter_context(nc.psum_tensor(f"ps{i}", [128, 512], F32)) for i in range(8)]
    Bps = [Buf(f"ps{i}", excl=True) for i in range(8)]

    scrA = Ring(es, nc, "scrA", (128, 640), F32, 4)
    colt = sb("colt", (64, 8, 9 * 8))
    Bcol = [Buf() for _ in range(8)]
    BSs = Buf("Ss")
    PH = {}
    Blnr_p = [Buf() for _ in range(3)]
    pending = []

    state = {"wt": 0}

    def wtile(l, key, nblk=16, pool_cast=False):
        t = TILE_ID[key]
        assert len(PLAN[t][1]) == nblk
        wb_, Bwb = wbf.next()
        hb = nblk // 2
        for hf in range(2):
            st, Bst = wst.next()
            n = hb * 128
            SP.dma(st[:, 0:n], w_dr[l][t, :, hf * n:(hf + 1) * n], [], [Bst], Bst)
            eng = A if (state["wt"] % 2 == 0) else V
            state["wt"] += 1
            src = st[:, 0:n].rearrange("p (k c) -> p k c", c=128)
            dst = wb_[:, hf * hb:(hf + 1) * hb, :]
            if pool_cast or eng is A:
                A.op(lambda: nc.scalar.activation(out=dst, in_=src, func=AF.Copy), [Bst], [Bwb])
            else:
                V.op(lambda: nc.vector.tensor_copy(out=dst, in_=src), [Bst], [Bwb])
        return wb_, Bwb

    class TS:
        def __init__(self, l, specs):
            self.l = l
            self.specs = specs
            self.i = 0
            self.pre = None

        def get(self, key):
            if self.pre is None:
                self.pre = wtile(self.l, *self.specs[0])
            assert self.specs[self.i][0] == key, (self.specs[self.i][0], key)
            cur = self.pre
            self.i += 1
            self.pre = wtile(self.l, *self.specs[self.i]) if self.i < len(self.specs) else None
            return cur

    def tblocks(nt):
        return [(0, PT)] + ([(PT, nt)] if nt > PT else [])

    def proj(wb_, Bwb, nk, rhs_fn, rhs_bufs, banks, nt):
        for bi, (t0, t1) in enumerate(tblocks(nt)):
            b = banks[bi]

            def fn(b=b, t0=t0, t1=t1):
                ins = None
                for k in range(nk):
                    ins = nc.tensor.matmul(ps[b][:, 0:t1 - t0], wb_[:, k, :], rhs_fn(k, t0, t1),
                                           start=(k == 0), stop=(k == nk - 1))
                return ins
            PE.op(fn, [Bwb] + rhs_bufs, [Bps[b]])

    def xb_rhs(k, t0, t1):
        return xb[:, k, t0:t1]

    pstate = {"sets": [(0, 1), (2, 3), (4, 5), (6, 7)], "i": 0}

    def pset():
        s = pstate["sets"][pstate["i"] % len(pstate["sets"])]
        pstate["i"] += 1
        return s

    def evac(eng, fn_mk, banks, nt, reads, writes):
        for bi, (t0, t1) in enumerate(tblocks(nt)):
            b = banks[bi]
            eng.op(lambda b=b, t0=t0, t1=t1: fn_mk(ps[b][:, 0:t1 - t0], t0, t1), [Bps[b]] + reads, writes)

    Bld = Buf("ld")
    SP.dma(cst[:], cst_dr, [], [Bcst], Bcst)
    SP.dma(wba_f[:], wba_dr, [], [Bwba], Bwba)
    SP.dma(cwa[:].rearrange("p a b -> p (a b)"), cwa_dr, [], [Bpar], Bld)
    SP.dma(cwg[:].rearrange("p a b -> p (a b)"), cwg_dr, [], [Bpar], Bld)
    SP.dma(nw[:], nw_dr, [], [Bpar], Bld)
    SP.dma(lnp[:].rearrange("p a b -> p (a b)"), lnp_dr, [], [Bpar], Bld)
    SP.dma(alog_bc[:], alog_dr, [], [Bpar], Bld)
    SP.dma(dtb_bc[:].rearrange("p a b -> p (a b)"), dtb_dr, [], [Bpar], Bld)
    G.op(lambda: nc.gpsimd.tensor_copy(out=wba_b[:].rearrange("p a b -> p (a b)"), in_=wba_f[:]), [Bwba], [Bwba])
    G.op(lambda: nc.gpsimd.memset(ones128[:], 1.0), [], [Bones])
    G.op(lambda: nc.gpsimd.tensor_copy(out=ident_b[:], in_=ident), [Bcst], [Bones])
    G.op(lambda: nc.gpsimd.memset(ones_b[:], 1.0), [], [Bones])
    G.op(lambda: nc.gpsimd.memset(Sp[:], 0.0), [], [b for r in BSp for b in r])
    G.op(lambda: nc.gpsimd.memset(ha[:], 0.0), [], [b for r in Bha for b in r])
    G.op(lambda: nc.gpsimd.memset(hg[:], 0.0), [], [b for r in Bhg for b in r])
    A.op(lambda: nc.scalar.activation(out=nA_bc[:].rearrange("p a b -> p (a b)"), in_=alog_bc[:], func=AF.Exp), [Bpar], [Bpar])
    V.op(lambda: nc.vector.tensor_scalar(out=nA_bc[:].rearrange("p a b -> p (a b)"), in0=nA_bc[:].rearrange("p a b -> p (a b)"),
                                          scalar1=-1.0, scalar2=None, op0=ALU.mult), [Bpar], [Bpar])

    def layer_norm(nt, g_col, b_col):
        blocks = tblocks(nt)
        for bi, (t0, t1) in enumerate(blocks):
            def fn(t0=t0, t1=t1, bi=bi):
                ins = None
                for kk in range(16):
                    ins = nc.tensor.matmul(ps[bi][:, 0:t1 - t0], ones128[:], xT[:, kk, t0:t1], start=(kk == 0), stop=(kk == 15))
                return ins
            PE.op(fn, [Bones] + BxT, [Bps[bi]])
        for k in range(16):
            s_, Bs_ = scrA.next()
            A.op(lambda s_=s_, k=k: nc.scalar.activation(out=s_[:, 0:nt], in_=xT[:, k, 0:nt], func=AF.Square),
                 [BxT[k]], [Bs_])
            for bi, (t0, t1) in enumerate(blocks):
                PE.op(lambda s_=s_, k=k, t0=t0, t1=t1, bi=bi: nc.tensor.matmul(ps[2 + bi][:, 0:t1 - t0], ones128[:], s_[:, t0:t1],
                                                                               start=(k == 0), stop=(k == 15), skip_group_check=True),
                      [Bones, Bs_], [Bps[2 + bi]])
        mean, msq, rstd = PH["lnr"]
        Blnr = PH["Blnr"]
        for bi, (t0, t1) in enumerate(blocks):
            n = t1 - t0
            A.op(lambda: nc.scalar.activation(out=mean[:, t0:t1], in_=ps[bi][:, 0:n], func=AF.Copy, scale=1.0 / D), [Bps[bi]], [Blnr[0]])
            V.op(lambda: nc.vector.tensor_tensor(out=msq[:, t0:t1], in0=mean[:, t0:t1], in1=mean[:, t0:t1], op=ALU.mult), [Blnr[0]], [Blnr[1]])
            V.op(lambda: nc.vector.scalar_tensor_tensor(out=rstd[:, t0:t1], in0=ps[2 + bi][:, 0:n], scalar=1.0 / D, in1=msq[:, t0:t1],
                                                        op0=ALU.mult, op1=ALU.subtract), [Bps[2 + bi], Blnr[1]], [Blnr[2]])
            A.op(lambda: nc.scalar.activation(out=rstd[:, t0:t1], in_=rstd[:, t0:t1], func=AF.Ln, bias=eps_ln[:, 0:1]), [Blnr[2], Bones], [Blnr[2]])
            A.op(lambda: nc.scalar.activation(out=rstd[:, t0:t1], in_=rstd[:, t0:t1], func=AF.Exp, scale=-0.5), [Blnr[2]], [Blnr[2]])
        for k in range(16):
            V.op(lambda k=k: nc.vector.tensor_tensor(out=xT[:, k, 0:nt], in0=xT[:, k, 0:nt], in1=mean[:, 0:nt], op=ALU.subtract),
                 [BxT[k], Blnr[0]], [BxT[k]])
            V.op(lambda k=k: nc.vector.tensor_tensor(out=xT[:, k, 0:nt], in0=xT[:, k, 0:nt], in1=rstd[:, 0:nt], op=ALU.mult),
                 [BxT[k], Blnr[2]], [BxT[k]])
            V.op(lambda k=k: nc.vector.tensor_scalar(out=xT[:, k, 0:nt], in0=xT[:, k, 0:nt], scalar1=g_col[:, k:k + 1], scalar2=b_col[:, k:k + 1],
                                                     op0=ALU.mult, op1=ALU.add), [BxT[k], Bpar], [BxT[k]])
            A.op(lambda k=k: nc.scalar.activation(out=xb[:, k, 0:nt], in_=xT[:, k, 0:nt], func=AF.Copy), [BxT[k]], [Bxb[k]])

    eps_ln = sb("eps_ln", (128, 3))
    G.op(lambda: nc.gpsimd.memset(eps_ln[:, 0:1], LN_EPS), [], [Bones])
    G.op(lambda: nc.gpsimd.memset(eps_ln[:, 1:2], NORM_EPS), [], [Bones])
    G.op(lambda: nc.gpsimd.memset(eps_ln[:, 2:3], 1.0), [], [Bones])

    def conv(pc, Bpc, K, wcol, out, Bout, nt, ns):
        H = K - 1
        A.op(lambda: nc.scalar.activation(out=out[:, 0:PT], in_=pc[:, 0:PT], func=AF.Copy, scale=wcol[:, 0:1]),
             [Bpc, Bpar], [Bout])
        for j in range(1, K):
            V.op(lambda j=j: nc.vector.scalar_tensor_tensor(out=out[:, 0:PT], in0=pc[:, j:j + PT], scalar=wcol[:, j:j + 1], in1=out[:, 0:PT],
                                                            op0=ALU.mult, op1=ALU.add), [Bpc, Bpar, Bout], [Bout])
        if ns:
            pcs = pc[:, H + PT:H + PT + 16 * (H + 4)].rearrange("p (s w) -> p s w", w=H + 4)
            outs = out[:, PT:PT + NS].rearrange("p (s t) -> p s t", t=4)
            A.op(lambda: nc.scalar.activation(out=outs, in_=pcs[:, :, 0:4], func=AF.Copy, scale=wcol[:, 0:1]),
                 [Bpc, Bpar], [Bout])
            for j in range(1, K):
                V.op(lambda j=j: nc.vector.scalar_tensor_tensor(out=outs, in0=pcs[:, :, j:j + 4], scalar=wcol[:, j:j + 1], in1=outs,
                                                                op0=ALU.mult, op1=ALU.add), [Bpc, Bpar, Bout], [Bout])

    def fill_pc(pc, Bpc, K, banks, nt, ns, halo_ap, Bhalo, st_dr, ost_dr, via=None):
        H = K - 1
        G.op(lambda: nc.gpsimd.tensor_copy(out=pc[:, 0:H], in_=halo_ap), [Bhalo], [Bpc])
        if via is None:
            A.op(lambda: nc.scalar.activation(out=pc[:, H:H + PT], in_=ps[banks[0]][:, 0:PT], func=AF.Copy), [Bps[banks[0]]], [Bpc])
        else:
            via(pc[:, H:H + PT], 0)
        G.op(lambda: nc.gpsimd.tensor_copy(out=halo_ap, in_=pc[:, PT:PT + H]), [Bpc], [Bhalo])
        if ns:
            pcs = pc[:, H + PT:H + PT + 16 * (H + 4)].rearrange("p (s w) -> p s w", w=H + 4)
            pending.append(SP.dma(pcs[:, :, 0:H], st_dr.rearrange("p (s w) -> p s w", w=H), [], [Bpc], Bpc))
            if via is None:
                A.op(lambda: nc.scalar.activation(out=pcs[:, :, H:H + 4], in_=ps[banks[1]][:, 0:NS].rearrange("p (s t) -> p s t", t=4),
                                                  func=AF.Copy), [Bps[banks[1]]], [Bpc])
            else:
                via(pcs[:, :, H:H + 4], 1)
            pending.append(SP.dma(ost_dr.rearrange("p (s w) -> p s w", w=H), pcs[:, :, 4:4 + H], [Bpc], [], Bpc))

    def barrier():
        engs = (PE, V, A, G, SP)
        toks = [Tok(e.sem, e.count) for e in engs if e.count > 0]
        for e in engs:
            for t in toks:
                if t.sem is not e.sem:
                    e.wait(t)
            for t in pending:
                e.wait(t)
        del pending[:]

    gst = {"i": 0, "GB": (4, 5, 6), "ob": 0}

    def gbank():
        GB = gst["GB"]
        b = GB[gst["i"] % len(GB)]
        gst["i"] += 1
        return b


    def gdn_cols(l, nt, ns):
        nch = 8 + (1 if ns else 0)
        w = nch * 8
        b = gbank()

        def fn():
            ins = None
            for c in range(nch):
                for k in range(16):
                    ins = nc.tensor.matmul(ps[b][0:64, c * 16:(c + 1) * 16], xb[:, k, c * 64:(c + 1) * 64], wba_b[:, l * 16 + k, :],
                                           start=(k == 0), stop=(k == 15))
            return ins
        PE.op(fn, Bxb + [Bwba], [Bps[b]])
        pv = ps[b][0:64, 0:nch * 16].rearrange("p (c x) -> p c x", x=16)

        def ct(i):
            return colt[:, i, 0:w]

        def ct3(i):
            return colt[:, i, 0:w].rearrange("p (c h) -> p c h", h=8)
        A.op(lambda: nc.scalar.activation(out=ct3(0), in_=pv[:, :, 0:8], func=AF.Sigmoid), [Bps[b]], [Bcol[0]])
        V.op(lambda: nc.vector.tensor_tensor(out=ct3(5), in0=pv[:, :, 8:16], in1=dtb_bc[:, l:l + 1, :].to_broadcast([64, nch, 8]), op=ALU.add),
             [Bps[b], Bpar], [Bcol[5]])
        G.op(lambda: nc.gpsimd.tensor_scalar(out=ct(6), in0=ct(5), scalar1=-1.0, scalar2=None, op0=ALU.mult), [Bcol[5]], [Bcol[6]])
        V.op(lambda: nc.vector.tensor_tensor(out=ct(6), in0=ct(6), in1=ct(5), op=ALU.min), [Bcol[5], Bcol[6]], [Bcol[6]])
        A.op(lambda: nc.scalar.activation(out=ct(6), in_=ct(6), func=AF.Exp), [Bcol[6]], [Bcol[6]])
        A.op(lambda: nc.scalar.activation(out=ct(6), in_=ct(6), func=AF.Ln, bias=eps_ln[0:64, 2:3]), [Bcol[6], Bones], [Bcol[6]])
        V.op(lambda: nc.vector.scalar_tensor_tensor(out=ct(5), in0=ct(5), scalar=0.0, in1=ct(6), op0=ALU.max, op1=ALU.add),
             [Bcol[5], Bcol[6]], [Bcol[5]])
        V.op(lambda: nc.vector.tensor_tensor(out=ct3(1), in0=ct3(5), in1=nA_bc[:, l:l + 1, :].to_broadcast([64, nch, 8]), op=ALU.mult),
             [Bcol[5], Bpar], [Bcol[1]])
        b2 = gbank()

        def fn2():
            nc.tensor.matmul(ps[b2][0:64, 0:64], Ltri, colt[:, 1, 0:64], start=True, stop=True)
            ins = nc.tensor.matmul(ps[b2][0:64, 128:192], ONES64, colt[:, 1, 0:64], start=True, stop=True, skip_group_check=True)
            if ns:
                nc.tensor.matmul(ps[b2][0:64, 64:72], Lblk, colt[:, 1, 64:72], start=True, stop=True, skip_group_check=True)
                ins = nc.tensor.matmul(ps[b2][0:64, 192:200], BLK, colt[:, 1, 64:72], start=True, stop=True, skip_group_check=True)
            return ins
        PE.op(fn2, [Bcol[1], Bcst], [Bps[b2]])
        A.op(lambda: nc.scalar.activation(out=ct(2), in_=ps[b2][0:64, 0:w], func=AF.Copy), [Bps[b2]], [Bcol[2]])
        V.op(lambda: nc.vector.tensor_tensor(out=ct(3), in0=ps[b2][0:64, 128:128 + w], in1=ct(2), op=ALU.subtract), [Bps[b2], Bcol[2]], [Bcol[3]])
        A.op(lambda: nc.scalar.activation(out=ct(3), in_=ct(3), func=AF.Exp), [Bcol[3]], [Bcol[3]])
        A.op(lambda: nc.scalar.activation(out=ct(4), in_=ct(2), func=AF.Exp), [Bcol[2]], [Bcol[4]])
        V.op(lambda: nc.vector.tensor_tensor(out=ct(4), in0=ct(4), in1=ct(0), op=ALU.mult), [Bcol[4], Bcol[0]], [Bcol[4]])

    def gdn_batch(l, h, c0, nb, t0, sample, R):
        qnb, knb, vsb = R["qkv"]
        Bqnb, Bknb, Bv = R["Bqkv"]
        Bq, Bk = Bqnb, Bknb
        qn, kn = qnb, knb
        szt, Bsz, ogT, BogT = R["szt"], R["Bsz"], PH["ogT"], PH["BogT"]
        BK = R["BK"]
        OB = BK[1]
        bst = {"scan": False}

        def gbank():
            if bst["scan"]:
                return BK[0]
            R["bi"] += 1
            return BK[R["bi"] % 2]

        def nb_(name):
            return R["gn"][name]
        W = nb * 64
        T1 = t0 + W
        Lm = Lblk if sample else Ltri
        negm = negm_blk if sample else negm_tri
        smk = sm_blk if sample else sm_tri

        def col(i):
            return colt[:, i, 0:72].rearrange("p (c h) -> p c h", h=8)[:, c0:c0 + nb, h]

        def v3(ap, inner):
            return ap.rearrange("p (c x) -> p c x", x=inner)

        Gbc, BG = nb_("Gbc")
        Bbc, BB = nb_("Bbc")
        A.op(lambda: nc.scalar.activation(out=v3(Gbc[0:64, 0:W], 64), in_=Lm.unsqueeze(1).to_broadcast([64, nb, 64]), func=AF.Copy), [Bcst], [BG])
        V.op(lambda: nc.vector.tensor_tensor(out=v3(Gbc[0:64, 0:W], 64), in0=v3(Gbc[0:64, 0:W], 64),
                                             in1=col(1).unsqueeze(2).to_broadcast([64, nb, 64]), op=ALU.mult), [BG, Bcol[1]], [BG])
        V.op(lambda: nc.vector.tensor_tensor(out=v3(Bbc[0:64, 0:W], 64), in0=I64.unsqueeze(1).to_broadcast([64, nb, 64]),
                                             in1=col(0).unsqueeze(2).to_broadcast([64, nb, 64]), op=ALU.mult), [Bcst, Bcol[0]], [BB])
        br = gbank()

        def fn():
            nc.tensor.matmul(ps[br][:, 0:W], ONES128a, Gbc[0:64, 0:W], start=True, stop=True, skip_group_check=True)
            return nc.tensor.matmul(ps[br][:, 256:256 + W], ONES128a, Bbc[0:64, 0:W], start=True, stop=True, skip_group_check=True)
        PE.op(fn, [BG, BB, Bcst, Bones], [Bps[br]])
        yield
        bkt = gbank()

        def fn6():
            ins = None
            for ci in range(nb):
                tk0 = t0 + ci * 64
                ins = nc.tensor.matmul(ps[bkt][0:64, ci * 128:(ci + 1) * 128], knb[:, tk0:tk0 + 64], ident_b[:], start=True, stop=True,
                                       skip_group_check=True)
            return ins
        PE.op(fn6, [Bk, Bones], [Bps[bkt]])
        yield
        egrow, Beg = nb_("egrow")
        kbT, Bkb = nb_("kbT")
        qdT, Bqd = nb_("qdT")
        tt, Btt = nb_("tt")
        decT, Bdec = nb_("decT")
        decTs, Bdecs = nb_("decTs")
        A.op(lambda: nc.scalar.activation(out=egrow[:, 0:W], in_=ps[br][:, 0:W], func=AF.Exp), [Bps[br]], [Beg])
        V.op(lambda: nc.vector.tensor_tensor(out=kbT[:, 0:W], in0=ps[br][:, 256:256 + W], in1=kn[:, t0:T1], op=ALU.mult), [Bps[br], Bk], [Bkb])
        V.op(lambda: nc.vector.tensor_tensor(out=v3(tt[0:64, 0:W], 64), in0=v3(ps[br][0:64, 0:W], 64),
                                             in1=col(2).unsqueeze(2).to_broadcast([64, nb, 64]), op=ALU.subtract), [Bps[br], Bcol[2]], [Btt])
        V.op(lambda: nc.vector.scalar_tensor_tensor(out=v3(tt[0:64, 0:W], 64), in0=v3(tt[0:64, 0:W], 64), scalar=0.0,
                                                    in1=negm.unsqueeze(1).to_broadcast([64, nb, 64]), op0=ALU.min, op1=ALU.add), [Btt, Bcst], [Btt])
        yield
        A.op(lambda: nc.scalar.activation(out=decT[0:64, 0:W], in_=tt[0:64, 0:W], func=AF.Exp), [Btt], [Bdec])
        V.op(lambda: nc.vector.tensor_tensor(out=qdT[:, 0:W], in0=qn[:, t0:T1], in1=egrow[:, 0:W], op=ALU.mult), [Bq, Beg], [Bqd])
        yield
        V.op(lambda: nc.vector.tensor_tensor(out=v3(decTs[0:64, 0:W], 64), in0=v3(decT[0:64, 0:W], 64),
                                             in1=smk.unsqueeze(1).to_broadcast([64, nb, 64]), op=ALU.mult), [Bdec, Bcst], [Bdecs])
        kbg, Bkbg = nb_("kbg")
        kd, Bkd = nb_("kd")
        vb, Bvb = nb_("vb")
        V.op(lambda: nc.vector.tensor_tensor(out=v3(kbg[0:64, 0:nb * 128], 128), in0=v3(ps[bkt][0:64, 0:nb * 128], 128),
                                             in1=col(4).unsqueeze(2).to_broadcast([64, nb, 128]), op=ALU.mult), [Bps[bkt], Bcol[4]], [Bkbg])
        V.op(lambda: nc.vector.tensor_tensor(out=v3(kd[0:64, 0:nb * 128], 128), in0=v3(ps[bkt][0:64, 0:nb * 128], 128),
                                             in1=col(3).unsqueeze(2).to_broadcast([64, nb, 128]), op=ALU.mult), [Bps[bkt], Bcol[3]], [Bkd])
        yield
        bk = gbank()

        def fn4():
            ins = None
            for ci in range(nb):
                tk0 = t0 + ci * 64
                nc.tensor.matmul(ps[bk][0:64, ci * 64:(ci + 1) * 64], knb[:, tk0:tk0 + 64], kbT[:, ci * 64:(ci + 1) * 64], start=True, stop=True,
                                 skip_group_check=True)
                ins = nc.tensor.matmul(ps[bk][0:64, 256 + ci * 64:256 + (ci + 1) * 64], knb[:, tk0:tk0 + 64], qnb[:, tk0:tk0 + 64], start=True, stop=True,
                                       skip_group_check=True)
            return ins
        PE.op(fn4, [Bknb, Bqnb, Bkb], [Bps[bk]])
        yield
        bvt = gbank()

        def fn6b():
            ins = None
            for ci in range(nb):
                tk0 = t0 + ci * 64
                ins = nc.tensor.matmul(ps[bvt][0:64, ci * 128:(ci + 1) * 128], vsb[:, tk0:tk0 + 64], ident_b[:], start=True, stop=True,
                                       skip_group_check=True)
            return ins
        PE.op(fn6b, [Bv, Bones], [Bps[bvt]])
        yield
        MT, BMT = nb_("MT")
        MTb, BMTb = nb_("MTb")
        AT, BAT = nb_("AT")
        V.op(lambda: nc.vector.tensor_tensor(out=MT[0:64, 0:W], in0=ps[bk][0:64, 0:W], in1=decTs[0:64, 0:W], op=ALU.mult), [Bps[bk], Bdecs], [BMT])
        V.op(lambda: nc.vector.tensor_tensor(out=MTb[0:64, 0:W], in0=ps[bk][0:64, 0:W], in1=decTs[0:64, 0:W], op=ALU.mult), [Bps[bk], Bdecs], [BMTb])
        V.op(lambda: nc.vector.tensor_tensor(out=AT[0:64, 0:W], in0=ps[bk][0:64, 256:256 + W], in1=decT[0:64, 0:W], op=ALU.mult), [Bps[bk], Bdec], [BAT])
        V.op(lambda: nc.vector.tensor_tensor(out=v3(vb[0:64, 0:nb * 128], 128), in0=v3(ps[bvt][0:64, 0:nb * 128], 128),
                                             in1=col(0).unsqueeze(2).to_broadcast([64, nb, 128]), op=ALU.mult), [Bps[bvt], Bcol[0]], [Bvb])
        yield
        bm = gbank()

        def fn5():
            ins = None
            for ci in range(nb):
                ins = nc.tensor.transpose(ps[bm][0:64, ci * 64:(ci + 1) * 64], MT[0:64, ci * 64:(ci + 1) * 64], I64)
            return ins
        PE.op(fn5, [BMT, Bcst], [Bps[bm]])
        yield
        Mb, BMb = nb_("Mb")
        P32, BP32 = nb_("P32")
        Pb, BPb = nb_("Pb")
        TTb, BTT = nb_("MTb")
        A.op(lambda: nc.scalar.activation(out=Mb[0:64, 0:W], in_=ps[bm][0:64, 0:W], func=AF.Copy), [Bps[bm]], [BMb])
        V.op(lambda: nc.vector.scalar_tensor_tensor(out=v3(Pb[0:64, 0:W], 64), in0=v3(MT[0:64, 0:W], 64), scalar=-1.0,
                                                    in1=I64.unsqueeze(1).to_broadcast([64, nb, 64]), op0=ALU.mult, op1=ALU.add), [BMT, Bcst], [BPb])
        V.op(lambda: nc.vector.scalar_tensor_tensor(out=v3(P32[0:64, 0:W], 64), in0=v3(MT[0:64, 0:W], 64), scalar=-1.0,
                                                    in1=I64.unsqueeze(1).to_broadcast([64, nb, 64]), op0=ALU.mult, op1=ALU.add), [BMT, Bcst], [BP32])
        nst = 1 if sample else 5

        def sq(lhs_a, rhs_a, Bl, Br):
            b_ = gbank()

            def f():
                ins = None
                for ci in range(nb):
                    sl = slice(ci * 64, (ci + 1) * 64)
                    ins = nc.tensor.matmul(ps[b_][0:64, sl], lhs_a[0:64, sl], rhs_a[0:64, sl], start=True, stop=True, skip_group_check=True)
                return ins
            PE.op(f, [Bl, Br], [Bps[b_]])
            return b_
        yield
        b1 = sq(Mb, MTb, BMb, BMTb)
        yield
        b2_ = sq(MTb, Mb, BMTb, BMb)
        yield
        Acur, BAc = nb_("A0")
        Atc, BAtc = nb_("At0")
        A.op(lambda: nc.scalar.activation(out=Acur[0:64, 0:W], in_=ps[b1][0:64, 0:W], func=AF.Copy), [Bps[b1]], [BAc])
        V.op(lambda: nc.vector.tensor_copy(out=Atc[0:64, 0:W], in_=ps[b2_][0:64, 0:W]), [Bps[b2_]], [BAtc])
        yield
        for n in range(1, nst + 1):
            bp = sq(Atc, Pb, BAtc, BPb)
            yield
            if n == nst:
                V.op(lambda bp=bp: nc.vector.tensor_tensor(out=TTb[0:64, 0:W], in0=ps[bp][0:64, 0:W], in1=P32[0:64, 0:W], op=ALU.add), [Bps[bp], BP32], [BTT])
            else:
                V.op(lambda bp=bp: nc.vector.tensor_tensor(out=Pb[0:64, 0:W], in0=ps[bp][0:64, 0:W], in1=P32[0:64, 0:W], op=ALU.add), [Bps[bp], BP32], [BPb])
                V.op(lambda bp=bp: nc.vector.tensor_tensor(out=P32[0:64, 0:W], in0=ps[bp][0:64, 0:W], in1=P32[0:64, 0:W], op=ALU.add), [Bps[bp], BP32], [BP32])
                ba = sq(Atc, Acur, BAtc, BAc)
                yield
                bat = sq(Acur, Atc, BAc, BAtc)
                yield
                An, BAn = nb_("A1" if n % 2 == 1 else "A0")
                Atn, BAtn = nb_("At1" if n % 2 == 1 else "At0")
                A.op(lambda ba=ba, An=An: nc.scalar.activation(out=An[0:64, 0:W], in_=ps[ba][0:64, 0:W], func=AF.Copy), [Bps[ba]], [BAn])
                V.op(lambda bat=bat, Atn=Atn: nc.vector.tensor_copy(out=Atn[0:64, 0:W], in_=ps[bat][0:64, 0:W]), [Bps[bat]], [BAtn])
                Acur, BAc, Atc, BAtc = An, BAn, Atn, BAtn
            yield
        bw = gbank()

        def fn7():
            ins = None
            for ci in range(nb):
                ins = nc.tensor.matmul(ps[bw][:, ci * 64:(ci + 1) * 64], kbg[0:64, ci * 128:(ci + 1) * 128], TTb[0:64, ci * 64:(ci + 1) * 64],
                                       start=True, stop=True, skip_group_check=True)
            return ins
        PE.op(fn7, [Bkbg, BTT], [Bps[bw]])
        yield
        wT, BwT = nb_("wT")
        A.op(lambda: nc.scalar.activation(out=wT[:, 0:W], in_=ps[bw][:, 0:W], func=AF.Copy), [Bps[bw]], [BwT])
        bu = gbank()

        def fn7b():
            ins = None
            for ci in range(nb):
                ins = nc.tensor.matmul(ps[bu][0:64, ci * 128:(ci + 1) * 128], TTb[0:64, ci * 64:(ci + 1) * 64], vb[0:64, ci * 128:(ci + 1) * 128],
                                       start=True, stop=True, skip_group_check=True)
            return ins
        PE.op(fn7b, [BTT, Bvb], [Bps[bu]])
        yield
        u, Bu = nb_("Gbc")
        A.op(lambda: nc.scalar.activation(out=u[0:64, 0:nb * 128], in_=ps[bu][0:64, 0:nb * 128], func=AF.Copy), [Bps[bu]], [Bu])
        yield
        bst["scan"] = True
        if not sample:
            S = Sp[:, l * 8 + h, :]
            BS = BSp[l][h]
            Sb = Spb[:, h, :]
            BSb = BSpb[h]
            for ci in range(nb):
                bws = gbank()
                PE.op(lambda: nc.tensor.matmul(ps[bws][0:64, 0:128], wT[:, ci * 64:(ci + 1) * 64], Sb, start=True, stop=True), [BwT, BSb], [Bps[bws]])
                yield
                vn, Bvn = nb_("A0" if ci % 2 == 0 else "At0")
                V.op(lambda: nc.vector.scalar_tensor_tensor(out=vn[0:64, 0:128], in0=ps[bws][0:64, 0:128], scalar=-1.0,
                                                            in1=u[0:64, ci * 128:(ci + 1) * 128], op0=ALU.mult, op1=ALU.add), [Bps[bws], Bu], [Bvn])
                yield

                def fo():
                    nc.tensor.matmul(ps[OB][0:64, ci * 128:(ci + 1) * 128], qdT[:, ci * 64:(ci + 1) * 64], Sb, start=True, stop=False,
                                     skip_group_check=True)
                    return nc.tensor.matmul(ps[OB][0:64, ci * 128:(ci + 1) * 128], AT[0:64, ci * 64:(ci + 1) * 64], vn[0:64, 0:128],
                                            start=False, stop=True, skip_group_check=True)
                PE.op(fo, [Bqd, BSb, BAT, Bvn], [Bps[OB]])
                yield
                bs_ = gbank()
                PE.op(lambda: nc.tensor.matmul(ps[bs_][:, 0:128], kd[0:64, ci * 128:(ci + 1) * 128], vn[0:64, 0:128], start=True, stop=True),
                      [Bkd, Bvn], [Bps[bs_]])
                yield
                V.op(lambda: nc.vector.scalar_tensor_tensor(out=Sb, in0=S, scalar=egrow[:, ci * 64 + 63:ci * 64 + 64], in1=ps[bs_][:, 0:128],
                                                            op0=ALU.mult, op1=ALU.add), [BS, Beg, Bps[bs_]], [BSb])
                V.op(lambda: nc.vector.scalar_tensor_tensor(out=S, in0=S, scalar=egrow[:, ci * 64 + 63:ci * 64 + 64], in1=ps[bs_][:, 0:128],
                                                            op0=ALU.mult, op1=ALU.add), [BS, Beg, Bps[bs_]], [BS])
                yield
        else:
            Ss, Ssb = PH["Ss"], PH["Ssb"]
            BSsb = PH["BSsb"]
            pending.append(SP.dma(Ss[:].rearrange("p s v -> p (s v)"), sg_dr[l, h], [], [BSs], BSs))
            A.op(lambda: nc.scalar.activation(out=Ssb[:], in_=Ss[:], func=AF.Copy), [BSs], [BSsb])
            wTm, qdm, kdm = PH["wTm"], PH["qdm"], PH["kdm"]
            Bmsk = [PH["BwTm"], PH["Bkdm"]]
            Bmsk3 = PH["Bqdm"]
            G.op(lambda: nc.gpsimd.tensor_tensor(out=wTm[:, :, 0:64], in0=wT[:, 0:64].unsqueeze(1).to_broadcast([128, 16, 64]),
                                                 in1=blockind.rearrange("p (s i) -> p s i", i=64), op=ALU.mult), [BwT, Bcst], [Bmsk[0]])
            G.op(lambda: nc.gpsimd.tensor_tensor(out=qdm[:], in0=qdT[:, 0:64].unsqueeze(1).to_broadcast([128, 16, 64]),
                                                 in1=blockind.rearrange("p (s i) -> p s i", i=64), op=ALU.mult), [Bqd, Bcst], [Bmsk3])
            G.op(lambda: nc.gpsimd.tensor_tensor(out=kdm[0:64, :, :], in0=kd[0:64, 0:128].unsqueeze(1).to_broadcast([64, 16, 128]),
                                                 in1=rowind.unsqueeze(2).to_broadcast([64, 16, 128]), op=ALU.mult), [Bkd, Bcst], [Bmsk[1]])
            bws = gbank()

            def fws():
                ins = None
                for s in range(16):
                    ins = nc.tensor.matmul(ps[bws][0:64, 0:128], wTm[:, s, 0:64], Ssb[:, s, :], start=(s == 0), stop=(s == 15))
                return ins
            PE.op(fws, [Bmsk[0], BSsb], [Bps[bws]])
            yield
            vn, Bvn = nb_("A0")
            V.op(lambda: nc.vector.scalar_tensor_tensor(out=vn[0:64, 0:128], in0=ps[bws][0:64, 0:128], scalar=-1.0, in1=u[0:64, 0:128],
                                                        op0=ALU.mult, op1=ALU.add), [Bps[bws], Bu], [Bvn])

            def fo():
                for s in range(16):
                    nc.tensor.matmul(ps[OB][0:64, 0:128], qdm[:, s, :], Ssb[:, s, :], start=(s == 0), stop=False)
                return nc.tensor.matmul(ps[OB][0:64, 0:128], AT[0:64, 0:64], vn[0:64, 0:128], start=False, stop=True)
            PE.op(fo, [Bmsk3, BSsb, BAT, Bvn], [Bps[OB]])
            yield
            egl = egrow[:, 0:64].rearrange("p (s t) -> p s t", t=4)[:, :, 3:4]
            for g4 in range(4):
                bs_ = gbank()

                def fs(g4=g4, bs_=bs_):
                    ins = None
                    for j in range(4):
                        s = g4 * 4 + j
                        ins = nc.tensor.matmul(ps[bs_][:, j * 128:(j + 1) * 128], kdm[0:64, s, :], vn[0:64, 0:128], start=True, stop=True,
                                               skip_group_check=True)
                    return ins
                PE.op(fs, [Bmsk[1], Bvn], [Bps[bs_]])
                yield
                if g4 == 0:
                    G.op(lambda: nc.gpsimd.tensor_tensor(out=Ss[:], in0=Ss[:], in1=egl.to_broadcast([128, 16, 128]), op=ALU.mult), [BSs, Beg], [BSs])
                V.op(lambda g4=g4, bs_=bs_: nc.vector.tensor_tensor(out=Ss[:, g4 * 4:(g4 + 1) * 4, :], in0=ps[bs_][:, 0:512].rearrange("p (s v) -> p s v", v=128),
                                                                    in1=Ss[:, g4 * 4:(g4 + 1) * 4, :], op=ALU.add), [Bps[bs_], BSs], [BSs])
            pending.append(SP.dma(ogs_dr[l, h], Ss[:].rearrange("p s v -> p (s v)"), [BSs], [], BSs))
        yield
        osq, Bosq = nb_("Gbc")
        A.op(lambda: nc.scalar.activation(out=osq[0:64, 0:nb * 128], in_=ps[OB][0:64, 0:nb * 128], func=AF.Square), [Bps[OB]], [Bosq])
        yield
        rs, Brs = nb_("rs")
        V.op(lambda: nc.vector.tensor_reduce(out=rs[0:64, 0:nb], in_=v3(osq[0:64, 0:nb * 128], 128), axis=AX.X, op=ALU.add), [Bosq], [Brs])
        yield
        A.op(lambda: nc.scalar.activation(out=rs[0:64, 0:nb], in_=rs[0:64, 0:nb], func=AF.Ln, scale=1.0 / 128, bias=eps_ln[0:64, 1:2]), [Brs, Bones], [Brs])
        A.op(lambda: nc.scalar.activation(out=rs[0:64, 0:nb], in_=rs[0:64, 0:nb], func=AF.Exp, scale=-0.5), [Brs], [Brs])
        yield
        on, Bon = nb_("Bbc")
        V.op(lambda: nc.vector.tensor_tensor(out=v3(on[0:64, 0:nb * 128], 128), in0=v3(ps[OB][0:64, 0:nb * 128], 128),
                                             in1=rs[0:64, 0:nb].unsqueeze(2).to_broadcast([64, nb, 128]), op=ALU.mult), [Bps[OB], Brs], [Bon])
        bot = gbank()

        def fot():
            ins = None
            for ci in range(nb):
                ins = nc.tensor.transpose(ps[bot][:, ci * 64:(ci + 1) * 64], on[0:64, ci * 128:(ci + 1) * 128], I64)
            return ins
        PE.op(fot, [Bon, Bcst], [Bps[bot]])
        yield
        V.op(lambda: nc.vector.scalar_tensor_tensor(out=ogT[:, h, t0:T1], in0=ps[bot][:, 0:W], scalar=nw[:, l:l + 1], in1=szt[:, t0:T1],
                                                    op0=ALU.mult, op1=ALU.add if False else ALU.mult), [Bps[bot], Bpar, Bsz], [BogT[h]])

    def layer(l, p):
        ns = NS if p == 0 else 0
        nt = PT + ns
        with ExitStack() as mix:
            def sbm(stack, name, shape, dt=F32):
                return stack.enter_context(nc.sbuf_tensor(f"{name}_{l}_{p}", list(shape), dt))
            ogT = sbm(mix, "ogT", (128, 8, NTMAX), BF16)
            BogT = [Buf() for _ in range(8)]
            K = 2 if ns else 3
            GN = {"Gbc": (512, F32), "Bbc": (512, F32), "egrow": (256, F32), "kbT": (256, BF16), "qdT": (256, BF16), "tt": (256, F32),
                  "decT": (256, F32), "decTs": (256, F32), "MT": (256, F32), "MTb": (256, BF16), "Mb": (256, BF16), "AT": (256, BF16),
                  "P32": (256, F32), "Pb": (256, BF16), "A0": (256, BF16), "At0": (256, BF16), "A1": (256, BF16),
                  "At1": (256, BF16), "kbg": (512, BF16), "kd": (512, BF16), "vb": (512, BF16), "wT": (256, BF16), "rs": (16, F32)}
            with ExitStack() as gs:
                PH["ogT"], PH["BogT"] = ogT, BogT
                slots = []
                for k in range(K):
                    R = {"qkv": [sbm(gs, f"qkv{i}_{k}", (128, NTMAX), BF16) for i in range(3)], "Bqkv": [Buf() for _ in range(3)],
                         "szt": sbm(gs, f"szt{k}", (128, NTMAX), BF16), "Bsz": Buf(),
                         "gn": {nm: (sbm(gs, f"gn{nm}_{k}", (128, sz), dt_), Buf()) for nm, (sz, dt_) in GN.items()},
                         "BK": (2 + k, 5 + k), "bi": 0}
                    slots.append(R)
                if ns:
                    PH["Ss"] = sbm(gs, "Ss", (128, 16, 128))
                    PH["Ssb"] = sbm(gs, "Ssb", (128, 16, 128), BF16); PH["BSsb"] = Buf()
                    PH["wTm"] = sbm(gs, "wTm", (128, 16, 64), BF16); PH["BwTm"] = Buf()
                    PH["qdm"] = sbm(gs, "qdm", (128, 16, 64), BF16); PH["Bqdm"] = Buf()
                    PH["kdm"] = sbm(gs, "kdm", (128, 16, 128), BF16); PH["Bkdm"] = Buf()
                    pstate["sets"] = [(0, 1)]
                    gst["GB"] = (2, 3, 4, 5)
                else:
                    pstate["sets"] = [(0,), (1,)]
                    gst["GB"] = (2, 3, 4)
                ckpt("start")
                gdn_cols(l, nt, ns)
                ckpt("cols")

                slock = {"held": False}

                def head_gen(h, R):
                    qkv, Bqkv, szt, Bsz = R["qkv"], R["Bqkv"], R["szt"], R["Bsz"]
                    for xi, (fin, Bfin_) in enumerate(zip(qkv, Bqkv)):
                        wb_, Bwb = wtile(l, ("w_in", (O_Q, O_K, O_V)[xi] // 128 + h, 0), pool_cast=True)
                        bk_ = pset()
                        proj(wb_, Bwb, 16, xb_rhs, Bxb, bk_, nt)
                        ch = xi * 8 + h
                        pc, Bpc = scrA.next()
                        dst, Bd = scrA.next()
                        fill_pc(pc, Bpc, 4, bk_, nt, ns, hg[:, l * 24 + ch, :], Bhg[l][ch], sgc_dr[:, l * 24 + ch, :], ogcs_dr[:, l * 24 + ch, :])
                        conv(pc, Bpc, 4, cwg[:, l * 24 + ch, :], dst, Bd, nt, ns)
                        if xi == 2:
                            A.op(lambda dst=dst, fin=fin: nc.scalar.activation(out=fin[:, 0:nt], in_=dst[:, 0:nt], func=AF.Silu), [Bd], [Bfin_])
                        else:
                            A.op(lambda dst=dst: nc.scalar.activation(out=dst[:, 0:nt], in_=dst[:, 0:nt], func=AF.Silu), [Bd], [Bd])
                            sq_, Bsq = scrA.next()
                            A.op(lambda dst=dst, sq_=sq_: nc.scalar.activation(out=sq_[:, 0:nt], in_=dst[:, 0:nt], func=AF.Square), [Bd], [Bsq])
                            bk2 = pset()
                            for bi, (t0, t1) in enumerate(tblocks(nt)):
                                PE.op(lambda bi=bi, t0=t0, t1=t1, sq_=sq_, bk2=bk2: nc.tensor.matmul(ps[bk2[bi]][:, 0:t1 - t0], ones128[:], sq_[:, t0:t1], start=True, stop=True),
                                      [Bones, Bsq], [Bps[bk2[bi]]])
                            rr, Brr = scrA.next()
                            evac(A, lambda pa, t0, t1: nc.scalar.activation(out=rr[:, t0:t1], in_=pa, func=AF.Ln, bias=eps_ln[:, 1:2]), bk2, nt, [Bones], [Brr])
                            A.op(lambda rr=rr: nc.scalar.activation(out=rr[:, 0:nt], in_=rr[:, 0:nt], func=AF.Exp, scale=-0.5), [Brr], [Brr])
                            if xi == 0:
                                V.op(lambda dst=dst, rr=rr, fin=fin: nc.vector.scalar_tensor_tensor(out=fin[:, 0:nt], in0=dst[:, 0:nt], scalar=float(128 ** -0.5),
                                                                                                    in1=rr[:, 0:nt], op0=ALU.mult, op1=ALU.mult), [Bd, Brr], [Bfin_])
                            else:
                                V.op(lambda dst=dst, rr=rr, fin=fin: nc.vector.tensor_tensor(out=fin[:, 0:nt], in0=dst[:, 0:nt], in1=rr[:, 0:nt], op=ALU.mult), [Bd, Brr], [Bfin_])
                        yield (8 if xi == 2 else 12)
                    A.op(lambda: nc.scalar.activation(out=Spb[:, h, :], in_=Sp[:, l * 8 + h, :], func=AF.Copy), [BSp[l][h]], [BSpb[h]])
                    wb_, Bwb = wtile(l, ("w_in", O_Z // 128 + h, 0), pool_cast=True)
                    bk_ = pset()
                    proj(wb_, Bwb, 16, xb_rhs, Bxb, bk_, nt)
                    evac(A, lambda pa, t0, t1: nc.scalar.activation(out=szt[:, t0:t1], in_=pa, func=AF.Silu), bk_, nt, [], [Bsz])
                    yield 4
                    yield from gdn_batch(l, h, 0, 4, 0, False, R)
                    yield from gdn_batch(l, h, 4, 4, 256, False, R)
                    if ns:
                        while slock["held"]:
                            yield 1
                        slock["held"] = True
                        yield from gdn_batch(l, h, 8, 1, 512, True, R)
                        slock["held"] = False

                def head_gen_pair(h, R):
                    qkv, Bqkv, szt, Bsz = R["qkv"], R["Bqkv"], R["szt"], R["Bsz"]
                    banks = [(0, 1), (4, 7)] if ns else [(0,), (1,)]
                    tiles = [wtile(l, ("w_in", (O_Q, O_K)[xi] // 128 + h, 0), pool_cast=True) for xi in range(2)]
                    for xi in range(2):
                        proj(tiles[xi][0], tiles[xi][1], 16, xb_rhs, Bxb, banks[xi], nt)
                    pcs_, dsts_ = [], []
                    for xi in range(2):
                        ch = xi * 8 + h
                        pc, Bpc = scrA.items[2 * xi]
                        dst, Bd = scrA.items[2 * xi + 1]
                        pcs_.append((pc, Bpc)); dsts_.append((dst, Bd))
                        fill_pc(pc, Bpc, 4, banks[xi], nt, ns, hg[:, l * 24 + ch, :], Bhg[l][ch], sgc_dr[:, l * 24 + ch, :], ogcs_dr[:, l * 24 + ch, :])
                    for xi in range(2):
                        ch = xi * 8 + h
                        conv(pcs_[xi][0], pcs_[xi][1], 4, cwg[:, l * 24 + ch, :], dsts_[xi][0], dsts_[xi][1], nt, ns)
                    for xi in range(2):
                        dst, Bd = dsts_[xi]
                        A.op(lambda dst=dst: nc.scalar.activation(out=dst[:, 0:nt], in_=dst[:, 0:nt], func=AF.Silu), [Bd], [Bd])
                    for xi in range(2):
                        dst, Bd = dsts_[xi]
                        A.op(lambda dst=dst, xi=xi: nc.scalar.activation(out=sqb[xi][:, 0:nt], in_=dst[:, 0:nt], func=AF.Square), [Bd], [Bsqb[xi]])
                    for xi in range(2):
                        for bi, (t0, t1) in enumerate(tblocks(nt)):
                            PE.op(lambda xi=xi, bi=bi, t0=t0, t1=t1: nc.tensor.matmul(ps[banks[xi][bi]][:, 0:t1 - t0], ones_b[:], sqb[xi][:, t0:t1], start=True, stop=True),
                                  [Bones, Bsqb[xi]], [Bps[banks[xi][bi]]])
                    for xi in range(2):
                        rr, Brr = pcs_[xi]
                        evac(A, lambda pa, t0, t1, rr=rr: nc.scalar.activation(out=rr[:, t0:t1], in_=pa, func=AF.Ln, bias=eps_ln[:, 1:2]), banks[xi], nt, [Bones], [Brr])
                    for xi in range(2):
                        rr, Brr = pcs_[xi]
                        A.op(lambda rr=rr: nc.scalar.activation(out=rr[:, 0:nt], in_=rr[:, 0:nt], func=AF.Exp, scale=-0.5), [Brr], [Brr])
                    for xi in range(2):
                        rr, Brr = pcs_[xi]
                        dst, Bd = dsts_[xi]
                        fin, Bfin_ = qkv[xi], Bqkv[xi]
                        if xi == 0:
                            V.op(lambda: nc.vector.scalar_tensor_tensor(out=fin[:, 0:nt], in0=dst[:, 0:nt], scalar=float(128 ** -0.5),
                                                                        in1=rr[:, 0:nt], op0=ALU.mult, op1=ALU.mult), [Bd, Brr], [Bfin_])
                        else:
                            V.op(lambda: nc.vector.tensor_tensor(out=fin[:, 0:nt], in0=dst[:, 0:nt], in1=rr[:, 0:nt], op=ALU.mult), [Bd, Brr], [Bfin_])
                    yield 20
                    tv = wtile(l, ("w_in", O_V // 128 + h, 0), pool_cast=True)
                    tz = wtile(l, ("w_in", O_Z // 128 + h, 0), pool_cast=True)
                    proj(tv[0], tv[1], 16, xb_rhs, Bxb, banks[0], nt)
                    proj(tz[0], tz[1], 16, xb_rhs, Bxb, banks[1], nt)
                    ch = 2 * 8 + h
                    pc, Bpc = scrA.items[0]
                    dst, Bd = scrA.items[1]
                    fill_pc(pc, Bpc, 4, banks[0], nt, ns, hg[:, l * 24 + ch, :], Bhg[l][ch], sgc_dr[:, l * 24 + ch, :], ogcs_dr[:, l * 24 + ch, :])
                    evac(A, lambda pa, t0, t1: nc.scalar.activation(out=szt[:, t0:t1], in_=pa, func=AF.Silu), banks[1], nt, [], [Bsz])
                    conv(pc, Bpc, 4, cwg[:, l * 24 + ch, :], dst, Bd, nt, ns)
                    A.op(lambda: nc.scalar.activation(out=qkv[2][:, 0:nt], in_=dst[:, 0:nt], func=AF.Silu), [Bd], [Bqkv[2]])
                    A.op(lambda: nc.scalar.activation(out=Spb[:, h, :], in_=Sp[:, l * 8 + h, :], func=AF.Copy), [BSp[l][h]], [BSpb[h]])
                    yield 12
                    yield from gdn_batch(l, h, 0, 4, 0, False, R)
                    yield from gdn_batch(l, h, 4, 4, 256, False, R)
                    if ns:
                        for _ in gdn_batch(l, h, 8, 1, 512, True, R):
                            pass

                active = []
                free = [(k, k * (104 // K)) for k in range(K)]
                nxt = 0
                while nxt < 8 or active:
                    while free and nxt < 8:
                        k, vt0 = free.pop(0)
                        active.append([(head_gen if ns else head_gen_pair)(nxt, slots[k]), k, vt0])
                        nxt += 1
                    item = min(active, key=lambda it: it[2])
                    try:
                        w = next(item[0])
                        item[2] += (w or 1)
                    except StopIteration:
                        active.remove(item)
                        free.append((item[1], item[2]))
                barrier()
            ckpt("gdn")
            a_in = sbm(mix, "a_in", (128, 8, NTMAX), BF16)
            Ba_in = [Buf() for _ in range(8)]
            specs = []
            for cc_ in range(8):
                specs += [(("w_in", O_GB // 128 + cc_, 0), 16), (("w_in", O_GC // 128 + cc_, 0), 16), (("w_in", O_H // 128 + cc_, 0), 16)]
            for oc_ in range(16):
                specs += [(("w_a_out", oc_, 0), 8), (("w_b_out", oc_, 0), 8), (("w_in_ma", oc_, 0), 16), (("w_in_mb", oc_, 0), 16)]
            for oc_ in range(16):
                specs.append((("w_o", oc_, 0), 16))
            for fc_ in range(64):
                specs.append((("w_up", fc_, 0), 16))
            for oc_ in range(16):
                for g_ in range(4):
                    specs.append((("w_down", oc_, g_), 16))
            ts = TS(l, specs)
            PH["ts"] = ts
            pstate["sets"] = [(0, 1), (2, 3), (4, 5), (6, 7)]
            for cc in range(8):
                wgB, BwgB = ts.get(("w_in", O_GB // 128 + cc, 0))
                bB = pset()
                proj(wgB, BwgB, 16, xb_rhs, Bxb, bB, nt)
                wgC, BwgC = ts.get(("w_in", O_GC // 128 + cc, 0))
                bC = pset()
                proj(wgC, BwgC, 16, xb_rhs, Bxb, bC, nt)
                wh, Bwh = ts.get(("w_in", O_H // 128 + cc, 0))
                bH = pset()
                proj(wh, Bwh, 16, xb_rhs, Bxb, bH, nt)
                hsb, Bhsb = scrA.next()
                evac(A, lambda pa, t0, t1: nc.scalar.activation(out=hsb[:, t0:t1], in_=pa, func=AF.Copy), bH, nt, [], [Bhsb])
                pc, Bpc = scrA.next()

                def via(dst_ap, which, bC=bC, hsb=hsb, Bhsb=Bhsb, Bpc=Bpc):
                    if which == 0:
                        V.op(lambda: nc.vector.tensor_tensor(out=dst_ap, in0=ps[bC[0]][:, 0:PT], in1=hsb[:, 0:PT], op=ALU.mult), [Bps[bC[0]], Bhsb], [Bpc])
                    else:
                        V.op(lambda: nc.vector.tensor_tensor(out=dst_ap, in0=ps[bC[1]][:, 0:NS].rearrange("p (s t) -> p s t", t=4),
                                                             in1=hsb[:, PT:PT + NS].rearrange("p (s t) -> p s t", t=4), op=ALU.mult), [Bps[bC[1]], Bhsb], [Bpc])
                fill_pc(pc, Bpc, 3, None, nt, ns, ha[:, l * 8 + cc, :], Bha[l][cc], sca_dr[:, l * 8 + cc, :], ocas_dr[:, l * 8 + cc, :], via=via)
                cva, Bcva = scrA.next()
                conv(pc, Bpc, 3, cwa[:, l * 8 + cc, :], cva, Bcva, nt, ns)
                evac(V, lambda pa, t0, t1: nc.vector.tensor_tensor(out=a_in[:, cc, t0:t1], in0=pa, in1=cva[:, t0:t1], op=ALU.mult), bB, nt, [Bcva], [Ba_in[cc]])
            ckpt("mixA")
            with ExitStack() as mg:
                big = sbm(mg, "mrg", (128, 16, NTMAX), BF16)
                Bbig = [Buf() for _ in range(16)]
                for oc in range(16):
                    wa, Bwa = ts.get(("w_a_out", oc, 0))
                    bya = pset()
                    proj(wa, Bwa, 8, lambda k, t0, t1: a_in[:, k, t0:t1], Ba_in, bya, nt)
                    wbo, Bwbo = ts.get(("w_b_out", oc, 0))
                    byb = pset()
                    proj(wbo, Bwbo, 8, lambda k, t0, t1: ogT[:, k, t0:t1], BogT, byb, nt)
                    wma, Bwma = ts.get(("w_in_ma", oc, 0))
                    bma = pset()
                    proj(wma, Bwma, 16, xb_rhs, Bxb, bma, nt)
                    wmb, Bwmb = ts.get(("w_in_mb", oc, 0))
                    bmb = pset()
                    proj(wmb, Bwmb, 16, xb_rhs, Bxb, bmb, nt)
                    sa, Bsa = scrA.next()
                    sbb, Bsbb = scrA.next()
                    evac(A, lambda pa, t0, t1: nc.scalar.activation(out=sa[:, t0:t1], in_=pa, func=AF.Sigmoid), bma, nt, [], [Bsa])
                    evac(A, lambda pa, t0, t1: nc.scalar.activation(out=sbb[:, t0:t1], in_=pa, func=AF.Sigmoid), bmb, nt, [], [Bsbb])
                    evac(V, lambda pa, t0, t1: nc.vector.tensor_tensor(out=sa[:, t0:t1], in0=pa, in1=sa[:, t0:t1], op=ALU.mult), bya, nt, [Bsa], [Bsa])
                    evac(V, lambda pa, t0, t1: nc.vector.tensor_tensor(out=sbb[:, t0:t1], in0=pa, in1=sbb[:, t0:t1], op=ALU.mult), byb, nt, [Bsbb], [Bsbb])
                    V.op(lambda: nc.vector.tensor_tensor(out=big[:, oc, 0:nt], in0=sa[:, 0:nt], in1=sbb[:, 0:nt], op=ALU.add), [Bsa, Bsbb], [Bbig[oc]])
                for oc in range(16):
                    wo, Bwo = ts.get(("w_o", oc, 0))
                    bo = pset()
                    proj(wo, Bwo, 16, lambda k, t0, t1: big[:, k, t0:t1], Bbig, bo, nt)
                    evac(V, lambda pa, t0, t1: nc.vector.scalar_tensor_tensor(out=xT[:, oc, t0:t1], in0=xT[:, oc, t0:t1], scalar=ALPHA, in1=pa,
                                                                              op0=ALU.mult, op1=ALU.add), bo, nt, [BxT[oc]], [BxT[oc]])
        ckpt("wo")
        PH["lnr"] = lnr_p
        PH["Blnr"] = Blnr_p
        layer_norm(nt, lnp[:, (0 * L + l), :], lnp[:, (1 * L + l), :])
        barrier()
        ckpt("ln1")
        with ExitStack() as mlp:
            big = mlp.enter_context(nc.sbuf_tensor(f"hid_{l}_{p}", [128, 64, NTMAX], BF16))
            Bbig = [Buf() for _ in range(64)]
            for fc in range(64):
                wu, Bwu = PH["ts"].get(("w_up", fc, 0))
                bu_ = pset()
                proj(wu, Bwu, 16, xb_rhs, Bxb, bu_, nt)
                r_, Br_ = scrA.next()
                evac(A, lambda pa, t0, t1: nc.scalar.activation(out=r_[:, t0:t1], in_=pa, func=AF.Relu), bu_, nt, [], [Br_])
                V.op(lambda: nc.vector.tensor_tensor(out=big[:, fc, 0:nt], in0=r_[:, 0:nt], in1=r_[:, 0:nt], op=ALU.mult), [Br_], [Bbig[fc]])
            for oc in range(16):
                bd_ = pset()
                for g in range(4):
                    wb_, Bwb = PH["ts"].get(("w_down", oc, g))
                    for bi, (t0, t1) in enumerate(tblocks(nt)):
                        def fn(wb_=wb_, g=g, t0=t0, t1=t1, b=bd_[bi]):
                            ins = None
                            for k in range(16):
                                ins = nc.tensor.matmul(ps[b][:, 0:t1 - t0], wb_[:, k, :], big[:, g * 16 + k, t0:t1],
                                                       start=(g == 0 and k == 0), stop=(g == 3 and k == 15), skip_group_check=True)
                            return ins
                        PE.op(fn, [Bwb] + Bbig[g * 16:(g + 1) * 16], [Bps[bd_[bi]]])
                evac(V, lambda pa, t0, t1: nc.vector.scalar_tensor_tensor(out=xT[:, oc, t0:t1], in0=xT[:, oc, t0:t1], scalar=ALPHA, in1=pa,
                                                                          op0=ALU.mult, op1=ALU.add), bd_, nt, [BxT[oc]], [BxT[oc]])
        layer_norm(nt, lnp[:, (2 * L + l), :], lnp[:, (3 * L + l), :])
        barrier()
        ckpt("layer")

    Bxio = Buf("xio")
    stopped = False
    for p in range(NPASS):
        if stopped:
            break
        nt = NTMAX if p == 0 else PT
        SP.dma(xT[:, :, 0:nt], x_in[p], [], BxT, Bxio)
        for k in range(16):
            A.op(lambda k=k: nc.scalar.activation(out=xb[:, k, 0:nt], in_=xT[:, k, 0:nt], func=AF.Copy), [BxT[k]], [Bxb[k]])
        try:
            for l in range(L):
                layer(l, p)
        except StopBuild:
            stopped = True
        pending.append(SP.dma(y_out[p], xT[:, :, 0:nt], BxT, [], Bxio))
    Bfin = Buf("fin")
    if DBG.get("skip_fin"):
        SP.dma(ogp_dr, Sp[:].rearrange("p a b -> p (a b)"), [b for r in BSp for b in r], [], Bfin)
        SP.wait(Tok(Bfin.dsem, Bfin.dcount))
        SP.wait(Tok(Bxio.dsem, Bxio.dcount))
        return nc
    SP.dma(ocap_dr, ha[:].rearrange("p a b -> p (a b)"), [b for r in Bha for b in r], [], Bfin)
    SP.dma(ogcp_dr, hg[:].rearrange("p a b -> p (a b)"), [b for r in Bhg for b in r], [], Bfin)
    SP.dma(ogp_dr, Sp[:].rearrange("p a b -> p (a b)"), [b for r in BSp for b in r], [], Bfin)
    SP.wait(Tok(Bfin.dsem, Bfin.dcount))
    SP.wait(Tok(Bxio.dsem, Bxio.dcount))
    if BSs.dsem is not None:
        SP.wait(Tok(BSs.dsem, BSs.dcount))
    for t in pending:
        SP.wait(t)
    for (_, bb) in scrA.items:
        if bb.dsem is not None:
            SP.wait(Tok(bb.dsem, bb.dcount))
    return nc


def _fm(a):
    t = a.shape[0]
    return np.ascontiguousarray(a.T.reshape(16, 128, t).transpose(1, 0, 2))


def _consts():
    c = np.zeros((128, 128 + 9 * 64 + 16 + 16 * 64), np.float32)
    c[:, 0:128] = np.eye(128, dtype=np.float32)
    j = np.arange(64)[:, None]
    i = np.arange(64)[None, :]
    same = (j // 4) == (i // 4)
    o = 128
    mats = [(j <= i), same & (j <= i), np.ones((64, 64), bool), same]
    for m in mats:
        c[0:64, o:o + 64] = m.astype(np.float32); o += 64
    c[0:64, o:o + 64] = np.where(i >= j, 0.0, NEG); o += 64
    c[0:64, o:o + 64] = np.where(same & (i >= j), 0.0, NEG); o += 64
    c[0:64, o:o + 64] = (i > j).astype(np.float32); o += 64
    c[0:64, o:o + 64] = (same & (i > j)).astype(np.float32); o += 64
    c[:, o:o + 64] = 1.0; o += 64
    c[0:64, o:o + 16] = ((np.arange(64)[:, None] // 4) == np.arange(16)[None, :]).astype(np.float32); o += 16
    bi = ((np.arange(64)[None, :] // 4) == np.arange(16)[:, None]).astype(np.float32)
    c[:, o:o + 1024] = bi.reshape(1, 1024)
    return c


def _chunkvec(v, nch):
    sh = v.shape[:-1]
    return np.moveaxis(v.reshape(sh + (nch, 128)), -1, 0)


_NC_CACHE = {}


def kernel(x_prompt, x_sample, state_conv_a, state_gdn_conv, state_gdn, w_in, conv_a_w, gdn_conv_w, a_log, dt_bias,
           gdn_norm_w, w_a_out, w_b_out, w_o, ln1_g, ln1_b, w_up, w_down, ln2_g, ln2_b):
    f = lambda a: np.asarray(a, dtype=np.float32)
    x_prompt, x_sample, state_conv_a, state_gdn_conv, state_gdn = map(f, (x_prompt, x_sample, state_conv_a, state_gdn_conv, state_gdn))
    w_in, w_a_out, w_b_out, w_o, w_up, w_down = map(f, (w_in, w_a_out, w_b_out, w_o, w_up, w_down))
    wts = []
    for l in range(L):
        mats = {"w_in": w_in[l], "w_a_out": w_a_out[l], "w_b_out": w_b_out[l], "w_o": w_o[l], "w_up": w_up[l], "w_down": w_down[l],
                "w_in_ma": w_in[l][:, O_MA:O_MA + 2048], "w_in_mb": w_in[l][:, O_MB:O_MB + 2048]}
        arr = np.zeros((NTILES, 128, 2048), np.float32)
        for t, (mn, blks) in enumerate(PLAN):
            M = mats[mn]
            for bi, (rc, cc) in enumerate(blks):
                arr[t, :, bi * 128:(bi + 1) * 128] = M[rc * 128:(rc + 1) * 128, cc * 128:(cc + 1) * 128]
        wts.append(arr)
    wba = np.stack([w_in[l][:, O_B:O_B + 16].reshape(16, 128, 16).transpose(1, 0, 2) for l in range(L)], axis=1)
    wba = np.ascontiguousarray(wba).reshape(128, -1)
    cwa = np.ascontiguousarray(_chunkvec(f(conv_a_w), 8).transpose(0, 1, 3, 2)).reshape(128, -1)
    cwg = np.ascontiguousarray(_chunkvec(f(gdn_conv_w), 24).transpose(0, 1, 3, 2)).reshape(128, -1)
    nw = np.ascontiguousarray(f(gdn_norm_w).T)
    lnp = np.ascontiguousarray(np.stack([_chunkvec(f(v), 16) for v in (ln1_g, ln1_b, ln2_g, ln2_b)], axis=1)).reshape(128, -1)
    alog = np.ascontiguousarray(np.broadcast_to(f(a_log).reshape(1, -1), (64, L * 8)))
    dtb = np.ascontiguousarray(np.broadcast_to(f(dt_bias).reshape(1, -1), (64, L * 8)))
    cst = _consts()
    in_maps = []
    for c in range(NCORES):
        s = c % 4
        sl = slice(16 * c, 16 * c + 16)
        m = {}
        xs = x_sample[sl].reshape(64, D)
        for p in range(NPASS):
            xp = x_prompt[s, p * PT:(p + 1) * PT]
            if p == 0:
                xp = np.concatenate([xp, xs], axis=0)
            m[f"x{p}"] = _fm(xp)
        for l in range(L):
            m[f"wt{l}"] = wts[l]
        m["wba"] = wba; m["cwa"] = cwa; m["cwg"] = cwg; m["nw"] = nw; m["lnp"] = lnp; m["alog"] = alog; m["dtb"] = dtb; m["cst"] = cst
        sca = state_conv_a[:, sl].reshape(L, 16, 2, 8, 128).transpose(4, 0, 3, 1, 2)
        m["sca"] = np.ascontiguousarray(sca).reshape(128, -1)
        sgc = state_gdn_conv[:, sl].reshape(L, 16, 3, 24, 128).transpose(4, 0, 3, 1, 2)
        m["sgc"] = np.ascontiguousarray(sgc).reshape(128, -1)
        sg = state_gdn[:, sl].transpose(0, 2, 3, 1, 4)
        m["sg"] = np.ascontiguousarray(sg).reshape(L, 8, 128, 16 * 128)
        in_maps.append(m)
    if "nc" not in _NC_CACHE:
        _NC_CACHE["nc"] = build_nc()
    nc = _NC_CACHE["nc"]
    res = run_bass_kernel_spmd(nc, in_maps, core_ids=list(range(NCORES)))
    R = res.results
    y_prompt = np.zeros((4, 2048, D), np.float32)
    y_sample = np.zeros((128, 4, D), np.float32)
    nca_p = np.zeros((L, 4, 2, 1024), np.float32)
    ngc_p = np.zeros((L, 4, 3, 3072), np.float32)
    ng_p = np.zeros((L, 4, 8, 128, 128), np.float32)
    nca_s = np.zeros((L, 128, 2, 1024), np.float32)
    ngc_s = np.zeros((L, 128, 3, 3072), np.float32)
    ng_s = np.zeros((L, 128, 8, 128, 128), np.float32)

    def unfm(a):
        return a.transpose(2, 1, 0).reshape(a.shape[2], D)
    for c in range(NCORES):
        r = R[c]
        sl = slice(16 * c, 16 * c + 16)
        if c < 4:
            for p in range(NPASS):
                y_prompt[c, p * PT:(p + 1) * PT] = unfm(np.asarray(r[f"y{p}"])[:, :, 0:PT])
            nca_p[:, c] = np.asarray(r["ocap"]).reshape(128, L, 8, 2).transpose(1, 3, 2, 0).reshape(L, 2, 1024)
            ngc_p[:, c] = np.asarray(r["ogcp"]).reshape(128, L, 24, 3).transpose(1, 3, 2, 0).reshape(L, 3, 3072)
            ng_p[:, c] = np.asarray(r["ogp"]).reshape(128, L, 8, 128).transpose(1, 2, 0, 3)
        y_sample[sl] = unfm(np.asarray(r["y0"])[:, :, PT:PT + NS]).reshape(16, 4, D)
        nca_s[:, sl] = np.asarray(r["ocas"]).reshape(128, L, 8, 16, 2).transpose(1, 3, 4, 2, 0).reshape(L, 16, 2, 1024)
        ngc_s[:, sl] = np.asarray(r["ogcs"]).reshape(128, L, 24, 16, 3).transpose(1, 3, 4, 2, 0).reshape(L, 16, 3, 3072)
        ng_s[:, sl] = np.asarray(r["ogs"]).reshape(L, 8, 128, 16, 128).transpose(0, 3, 1, 2, 4)
    return (y_prompt, y_sample, nca_p, ngc_p, ng_p, nca_s, ngc_s, ng_s)
```
